# Optimizing a Trainium2 kernel written in Bass

```python
import jax, jax.numpy as jnp
from jax import lax
import numpy as np

D_MODEL = 1024
BATCH = 8
SEQ = 2048
DEPTH = 1

MLA_HEADS = 8
MLA_NOPE = 64
MLA_ROPE = 32
MLA_V = 64
MLA_QK = MLA_NOPE + MLA_ROPE
MLA_Q_RANK = 256
MLA_KV_RANK = 128
RET_HEADS = 4
RET_DK = 64
RET_DV = 128
CHUNK = 128
Q_BLOCK = 128
D_FF = 4 * D_MODEL
ROPE_BASE = 10000.0
EPS = 1e-5
MLA_WIDTH = MLA_HEADS * MLA_V
RET_WIDTH = RET_HEADS * RET_DV
MIX_WIDTH = MLA_WIDTH + RET_WIDTH
ALPHA = (2.0 * DEPTH) ** 0.25
BETA = (8.0 * DEPTH) ** -0.25
IN_SPLITS = (MLA_Q_RANK, MLA_KV_RANK, MLA_ROPE,
             RET_HEADS * RET_DK, RET_HEADS * RET_DK, RET_WIDTH, RET_WIDTH)
IN_WIDTH = sum(IN_SPLITS)
IN_OFFSETS = tuple(int(v) for v in np.cumsum(IN_SPLITS)[:-1])

kernel_name = "hymba_mla_retention_deepnorm_adaln"


def layer_norm(x, g, b):
    xf = x.astype(jnp.float32)
    mu = jnp.mean(xf, -1, keepdims=True)
    var = jnp.mean(jnp.square(xf - mu), -1, keepdims=True)
    return ((xf - mu) * lax.rsqrt(var + EPS) * g.astype(jnp.float32) + b.astype(jnp.float32)).astype(x.dtype)


def rms_norm(x, g):
    xf = x.astype(jnp.float32)
    ms = jnp.mean(jnp.square(xf), -1, keepdims=True)
    return (xf * lax.rsqrt(ms + EPS) * g.astype(jnp.float32)).astype(x.dtype)


def rotary(x, pos):
    half = x.shape[-1] // 2
    inv = ROPE_BASE ** (-jnp.arange(half, dtype=jnp.float32) / half)
    ang = pos.astype(jnp.float32)[..., None] * inv
    cos = jnp.cos(ang)[:, :, None, :]
    sin = jnp.sin(ang)[:, :, None, :]
    x1 = x[..., :half].astype(jnp.float32)
    x2 = x[..., half:].astype(jnp.float32)
    return jnp.concatenate([x1 * cos - x2 * sin, x2 * cos + x1 * sin], -1).astype(x.dtype)


def causal_block_attention(q, k, v):
    b, s, h, dq = q.shape
    dv = v.shape[-1]
    nb = s // Q_BLOCK
    scale = dq ** -0.5
    qb = q.reshape(b, nb, Q_BLOCK, h, dq).transpose(1, 0, 3, 2, 4)
    kt = k.transpose(0, 2, 1, 3)
    vt = v.transpose(0, 2, 1, 3)
    k_idx = jnp.arange(s)

    def one_block(args):
        q_blk, i = args
        sc = jnp.einsum('bhqd,bhkd->bhqk', q_blk, kt,
                        preferred_element_type=jnp.float32) * scale
        q_idx = i * Q_BLOCK + jnp.arange(Q_BLOCK)
        mask = k_idx[None, :] <= q_idx[:, None]
        p = jax.nn.softmax(jnp.where(mask, sc, -jnp.inf), axis=-1)
        return jnp.einsum('bhqk,bhkd->bhqd', p.astype(vt.dtype), vt)

    out = lax.map(one_block, (qb, jnp.arange(nb)))
    return out.transpose(1, 0, 3, 2, 4).reshape(b, s, h * dv)


def mla_mixer(q_c, kv_c, k_r, pos, g_q, w_uq, g_kv, w_ukv):
    b, s, _ = q_c.shape
    q = (rms_norm(q_c, g_q) @ w_uq).reshape(b, s, MLA_HEADS, MLA_QK)
    q = jnp.concatenate([q[..., :MLA_NOPE], rotary(q[..., MLA_NOPE:], pos)], -1)
    kv = (rms_norm(kv_c, g_kv) @ w_ukv).reshape(b, s, MLA_HEADS, MLA_NOPE + MLA_V)
    k_nope, v = kv[..., :MLA_NOPE], kv[..., MLA_NOPE:]
    k_rope = rotary(k_r[:, :, None, :], pos)
    k = jnp.concatenate([k_nope, jnp.broadcast_to(k_rope, (b, s, MLA_HEADS, MLA_ROPE))], -1)
    return causal_block_attention(q, k, v)


def retention_mixer(q, k, v, g, pos, gn_g, gn_b):
    b, s, _ = q.shape
    n = s // CHUNK
    q = rotary(q.reshape(b, s, RET_HEADS, RET_DK), pos)
    k = rotary(k.reshape(b, s, RET_HEADS, RET_DK), pos) * (RET_DK ** -0.5)
    v = v.reshape(b, s, RET_HEADS, RET_DV)
    qc = q.reshape(b, n, CHUNK, RET_HEADS, RET_DK).transpose(0, 3, 1, 2, 4)
    kc = k.reshape(b, n, CHUNK, RET_HEADS, RET_DK).transpose(0, 3, 1, 2, 4)
    vc = v.reshape(b, n, CHUNK, RET_HEADS, RET_DV).transpose(0, 3, 1, 2, 4)

    log_g = jnp.log(1.0 - 2.0 ** (-5.0 - jnp.arange(RET_HEADS, dtype=jnp.float32)))
    idx = jnp.arange(CHUNK, dtype=jnp.float32)
    diff = idx[:, None] - idx[None, :]
    decay = jnp.where(diff >= 0, jnp.exp(log_g[:, None, None] * jnp.maximum(diff, 0.0)), 0.0)
    zeta = jnp.exp(log_g[:, None] * (CHUNK - 1 - idx))
    xi = jnp.exp(log_g[:, None] * (idx + 1.0))

    scores = jnp.einsum('bhnid,bhnjd->bhnij', qc, kc) * decay[None, :, None]
    inner = jnp.einsum('bhnij,bhnjv->bhniv', scores, vc)
    chunk_kv = jnp.einsum('bhnjd,hj,bhnjv->bhndv', kc, zeta, vc)
    g_chunk = jnp.exp(log_g * CHUNK).astype(chunk_kv.dtype)[None, :, None, None]

    def step(state, kv_n):
        return state * g_chunk + kv_n, state

    init = jnp.zeros((b, RET_HEADS, RET_DK, RET_DV), chunk_kv.dtype)
    _, prev = lax.scan(step, init, chunk_kv.transpose(2, 0, 1, 3, 4))
    prev = prev.transpose(1, 2, 0, 3, 4)
    cross = jnp.einsum('bhnid,bhndv->bhniv', qc, prev) * xi[None, :, None, :, None]

    o = (inner + cross).transpose(0, 2, 3, 1, 4).reshape(b, s, RET_HEADS, RET_DV)
    of = o.astype(jnp.float32)
    mu = jnp.mean(of, -1, keepdims=True)
    var = jnp.mean(jnp.square(of - mu), -1, keepdims=True)
    on = ((of - mu) * lax.rsqrt(var + EPS)).reshape(b, s, RET_WIDTH)
    on = on * gn_g.astype(jnp.float32) + gn_b.astype(jnp.float32)
    return (on * jax.nn.silu(g.astype(jnp.float32))).astype(g.dtype)


def setup_inputs(seed: int = 0) -> dict:
    key = jax.random.key(seed)
    ks = jax.random.split(key, 24)
    f32 = jnp.float32
    nrm = lambda k, shape, fan: jax.random.normal(k, shape, f32) * (fan ** -0.5)
    gain = lambda k, shape: 1.0 + 0.05 * jax.random.normal(k, shape, f32)
    bias = lambda k, shape: 0.02 * jax.random.normal(k, shape, f32)

    x = jax.random.normal(ks[0], (BATCH, SEQ, D_MODEL), f32)
    c = jax.random.normal(ks[1], (BATCH, D_MODEL), f32)
    offset = jax.random.randint(ks[2], (BATCH, 1), 0, 1024, dtype=jnp.int32)
    positions = (offset + jnp.arange(SEQ, dtype=jnp.int32)[None, :]).astype(jnp.int32)

    in_col_scale = jnp.concatenate([
        jnp.ones((IN_WIDTH - 2 * RET_WIDTH,), f32),
        jnp.full((RET_WIDTH,), BETA, f32),
        jnp.ones((RET_WIDTH,), f32)])
    ukv_col_scale = jnp.tile(jnp.concatenate([jnp.ones((MLA_NOPE,), f32),
                                              jnp.full((MLA_V,), BETA, f32)]), MLA_HEADS)
    return {
        "x": x,
        "c": c,
        "positions": positions,
        "ln_in_g": gain(ks[3], (D_MODEL,)),
        "ln_in_b": bias(ks[4], (D_MODEL,)),
        "w_ada": nrm(ks[5], (DEPTH, D_MODEL, 6 * D_MODEL), D_MODEL) * 0.5,
        "b_ada": bias(ks[6], (DEPTH, 6 * D_MODEL)),
        "w_in": nrm(ks[7], (DEPTH, D_MODEL, IN_WIDTH), D_MODEL) * in_col_scale,
        "mla_q_norm": gain(ks[8], (DEPTH, MLA_Q_RANK)),
        "w_uq": nrm(ks[9], (DEPTH, MLA_Q_RANK, MLA_HEADS * MLA_QK), MLA_Q_RANK),
        "mla_kv_norm": gain(ks[10], (DEPTH, MLA_KV_RANK)),
        "w_ukv": nrm(ks[11], (DEPTH, MLA_KV_RANK, MLA_HEADS * (MLA_NOPE + MLA_V)), MLA_KV_RANK) * ukv_col_scale,
        "ret_gn_g": gain(ks[12], (DEPTH, RET_WIDTH)),
        "ret_gn_b": bias(ks[13], (DEPTH, RET_WIDTH)),
        "w_out": nrm(ks[14], (DEPTH, MIX_WIDTH, D_MODEL), MIX_WIDTH) * BETA,
        "ln1_g": gain(ks[15], (DEPTH, D_MODEL)),
        "ln1_b": bias(ks[16], (DEPTH, D_MODEL)),
        "w_ff1": nrm(ks[17], (DEPTH, D_MODEL, D_FF), D_MODEL) * BETA,
        "w_ff2": nrm(ks[18], (DEPTH, D_FF, D_MODEL), D_FF) * BETA,
        "ln2_g": gain(ks[19], (DEPTH, D_MODEL)),
        "ln2_b": bias(ks[20], (DEPTH, D_MODEL)),
    }


def reference(x, c, positions, ln_in_g, ln_in_b, w_ada, b_ada, w_in, mla_q_norm, w_uq,
              mla_kv_norm, w_ukv, ret_gn_g, ret_gn_b, w_out, ln1_g, ln1_b,
              w_ff1, w_ff2, ln2_g, ln2_b):
    x = layer_norm(x, ln_in_g, ln_in_b)
    c_act = jax.nn.silu(c)
    for l in range(DEPTH):
        mod = c_act @ w_ada[l] + b_ada[l]
        sh1, sc1, gt1, sh2, sc2, gt2 = [m[:, None, :] for m in jnp.split(mod, 6, axis=-1)]

        h = x * (1.0 + sc1) + sh1
        proj = h @ w_in[l]
        q_c, kv_c, k_r, r_q, r_k, r_v, r_g = jnp.split(proj, IN_OFFSETS, axis=-1)
        a = mla_mixer(q_c, kv_c, k_r, positions, mla_q_norm[l], w_uq[l], mla_kv_norm[l], w_ukv[l])
        r = retention_mixer(r_q, r_k, r_v, r_g, positions, ret_gn_g[l], ret_gn_b[l])
        y = jnp.concatenate([a, r], axis=-1) @ w_out[l]
        x = layer_norm(ALPHA * x + gt1 * y, ln1_g[l], ln1_b[l])

        h = x * (1.0 + sc2) + sh2
        f = jnp.square(jax.nn.relu(h @ w_ff1[l])) @ w_ff2[l]
        x = layer_norm(ALPHA * x + gt2 * f, ln2_g[l], ln2_b[l])
    return x
```

```python
import numpy as np
import concourse.bass as bass
import concourse.mybir as mybir
from concourse.ap import AP
from concourse.bass_utils import run_bass_kernel_spmd

F32 = mybir.dt.float32
BF = mybir.dt.bfloat16
I32 = mybir.dt.int32
AF = mybir.ActivationFunctionType
ALU = mybir.AluOpType

S = 2048
D = 1024
NT = 16
EPS = 1e-5
ALPHA = 2.0 ** 0.25
DFF = 4096
SB_BASE = 16640
COST_TABLE = {
    'act|P.add("act", lambda e, bank=bank, half=half, h=h: e.copy(out=KT[64:96,': 0.979,
    'act|P.add("act", lambda e, bank=bank, rs_=rs_: e.activation(out=rl[rs_][:,': 0.589,
    'act|P.add("act", lambda e, bank=bank, t=t: e.copy(out=QT[:, :, t * 128:(t ': 1.007,
    'act|P.add("act", lambda e, bank=bank, t=t: e.copy(out=h2T[:, :, t * 128:(t': 0.976,
    'act|P.add("act", lambda e, h=h, bank=bank, c=c: e.copy(out=KT[0:64, h, c *': 0.534,
    'act|P.add("act", lambda e, i=i, bank=bank, n=n: e.activation(': 0.604,
    'act|P.add("act", lambda e, mc=mc: e.activation(out=sq[mc][:, :], in_=ps[ba': 0.568,
    'act|P.add("act", lambda e, qv=qv, s2=s2, g=g: e.copy(out=Qtok[s2][:, g * 4': 0.28,
    'act|P.add("act", lambda e, t=t, bank=bank: e.copy(out=V[:, t, :, 0:64], in': 0.513,
    'act|P.add("act", lambda e: e.activation(out=PT[pi][:, lo:lo + n], in_=ps[b': 0.461,
    'act|P.add("act", lambda e: e.activation(out=c_act[:, :], in_=c_sb[:, :], f': 0.209,
    'act|P.add("act", lambda e: e.activation(out=dst, in_=rr[:, :, :], func=AF.': 0.523,
    'act|P.add("act", lambda e: e.activation(out=mv[s2][:, 2:3], in_=mv[s2][:, ': 0.295,
    'act|P.add("act", lambda e: e.activation(out=mv_p1[s2][:, 2:3], in_=mv_p1[s': 0.241,
    'act|P.add("act", lambda e: e.activation(out=rstd4[s2][:, :], in_=mv4[s2][:': 0.297,
    'act|P.add("act", lambda e: e.activation(out=sg[s2][:, :], in_=ps[b3][:, :]': 0.597,
    'act|P.add("act", lambda e: e.activation(out=xh_p1[s2][:, :], in_=xt[xs][:,': 1.234,
    'act|P.add("act", lambda e: e.activation(out=zt[s2][:, :], in_=zt[s2][:, :]': 1.24,
    'act|P.add("act", lambda e: e.copy(out=hT[cs_][:, :, tt * 128:(tt + 1) * 12': 0.926,
    'act|P.add("act", lambda e: e.copy(out=o_sb[s2][:, :], in_=ps[bo][0:64, :])': 0.556,
    'act|P.add("act", lambda e: e.copy(out=qkT[s2][:, :, :], in_=psb[bt][0:64, ': 0.915,
    'act|P.add("act", lambda e: e.copy(out=rT[:, :, t * 128:(t + 1) * 128], in_': 0.51,
    'act|P.add("act", lambda e: e.copy(out=state_bf[1 - sbi][:, :], in_=state[:': 0.534,
    'act|P.add("act", lambda e: e.copy(out=v_r[s2][:, :], in_=ps[b2][:, :]), re': 0.549,
    'dve|P.add("dve", lambda e, bank=bank, half=half, h=h: e.tensor_copy(out=KT': 0.689,
    'dve|P.add("dve", lambda e, h=h, bank=bank, c=c: e.tensor_copy(out=KT[0:64,': 0.64,
    'dve|P.add("dve", lambda e, h=h: e.bn_aggr(mv4[s2][:, h, :], st4[s2][:, h, ': 0.169,
    'dve|P.add("dve", lambda e, h=h: e.bn_stats(st4[s2][:, h, :], ps[bo][:, h *': 0.205,
    'dve|P.add("dve", lambda e, h=h: e.tensor_scalar(out=on[s2][:, h * 128:(h +': 0.342,
    'dve|P.add("dve", lambda e, hf=hf: e.tensor_tensor(out=zt[s2][:, hf * 512:(': 0.66,
    'dve|P.add("dve", lambda e, i=i: e.reciprocal(out=rs[i][:, :], in_=rs[i][:,': 3.265,
    'dve|P.add("dve", lambda e, inv=inv, ang=ang, hd=hd: e.tensor_tensor(': 0.512,
    'dve|P.add("dve", lambda e, mc=mc, g_ap=g_ap, ri=ri: e.scalar_tensor_tensor': 0.651,
    'dve|P.add("dve", lambda e, s2=s2: e.tensor_tensor(out=x1l[s2][:, :], in0=z': 1.226,
    'dve|P.add("dve", lambda e, s2=s2: e.tensor_tensor(out=x1t[s2][:, :], in0=z': 1.658,
    'dve|P.add("dve", lambda e: e.bn_aggr(mv[s2][:, 0:2], st[s2][:, :]), reads=': 0.209,
    'dve|P.add("dve", lambda e: e.bn_aggr(mv_p1[s2][:, 0:2], st_p1[s2][:, :]), ': 0.174,
    'dve|P.add("dve", lambda e: e.bn_stats(st[s2][:, 0:6], zt[s2][:, 0:512]), r': 0.694,
    'dve|P.add("dve", lambda e: e.bn_stats(st[s2][:, 6:12], zt[s2][:, 512:1024]': 0.591,
    'dve|P.add("dve", lambda e: e.bn_stats(st_p1[s2][:, 0:6], src[:, 0:512]), r': 0.615,
    'dve|P.add("dve", lambda e: e.bn_stats(st_p1[s2][:, 6:12], src[:, 512:1024]': 0.599,
    'dve|P.add("dve", lambda e: e.reciprocal(out=mv[s2][:, 2:3], in_=mv[s2][:, ': 0.133,
    'dve|P.add("dve", lambda e: e.reciprocal(out=mv_p1[s2][:, 2:3], in_=mv_p1[s': 0.096,
    'dve|P.add("dve", lambda e: e.reciprocal(out=rden[s2][64:65, :], in_=ps[bo]': 3.324,
    'dve|P.add("dve", lambda e: e.reciprocal(out=rstd4[s2][:, :], in_=rstd4[s2]': 0.172,
    'dve|P.add("dve", lambda e: e.scalar_tensor_tensor(out=mv[s2][:, 3:4], in0=': 0.699,
    'dve|P.add("dve", lambda e: e.scalar_tensor_tensor(out=mv_p1[s2][:, 3:4], i': 0.793,
    'dve|P.add("dve", lambda e: e.scalar_tensor_tensor(out=rr[:, :, :], in0=qq[': 0.559,
    'dve|P.add("dve", lambda e: e.scalar_tensor_tensor(out=zt[s2][:, :], in0=xl': 1.454,
    'dve|P.add("dve", lambda e: e.tensor_copy(out=cb[:, :], in_=c_act[:, :]), r': 0.079,
    'dve|P.add("dve", lambda e: e.tensor_copy(out=ki[:, :, :], in_=qq[:, :, :])': 0.361,
    'dve|P.add("dve", lambda e: e.tensor_copy(out=pos_f[:, :], in_=pos_i[:, :])': 0.175,
    'dve|P.add("dve", lambda e: e.tensor_copy(out=qq[:, :, :], in_=ki[:, :, :])': 0.36,
    'dve|P.add("dve", lambda e: e.tensor_copy(out=state[:, :], in_=ps[bc][0:64,': 0.598,
    'dve|P.add("dve", lambda e: e.tensor_scalar(out=Ab1[:, :], in0=Ab1[:, :], s': 0.693,
    'dve|P.add("dve", lambda e: e.tensor_scalar(out=modrow_[s2][:, :], in0=ps[b': 0.671,
    'dve|P.add("dve", lambda e: e.tensor_scalar(out=qq[:, :, :], in0=rr[:, :, :': 0.349,
    'dve|P.add("dve", lambda e: e.tensor_scalar(out=rr[:, :, :], in0=rr[:, :, :': 0.36,
    'dve|P.add("dve", lambda e: e.tensor_scalar(out=rr[:, :, :], in0=src[:, :, ': 0.36,
    'dve|P.add("dve", lambda e: e.tensor_tensor(out=G1[:, :], in0=g_in[:, :], i': 1.134,
    'dve|P.add("dve", lambda e: e.tensor_tensor(out=G2[:, :], in0=g1[:, :], in1': 1.137,
    'dve|P.add("dve", lambda e: e.tensor_tensor(out=H1[:, :], in0=htmp_p1[:, :]': 1.226,
    'dve|P.add("dve", lambda e: e.tensor_tensor(out=H2[:, :], in0=htmp_p4[:, :]': 1.224,
    'dve|P.add("dve", lambda e: e.tensor_tensor(out=aT[po:po + 64, h // 2, c * ': 0.68,
    'dve|P.add("dve", lambda e: e.tensor_tensor(out=htmp_p1[:, :], in0=b_in[:, ': 1.226,
    'dve|P.add("dve", lambda e: e.tensor_tensor(out=htmp_p4[:, :], in0=b1[:, :]': 1.224,
    'dve|P.add("dve", lambda e: e.tensor_tensor(out=qxi[s2][:, :, :], in0=psb[b': 0.67,
    'dve|P.add("dve", lambda e: e.tensor_tensor(out=rr[:, :, :], in0=rr[:, :, :': 0.558,
    'dve|P.add("dve", lambda e: e.tensor_tensor(out=scm[s2][:, :], in0=ps[bs][:': 0.603,
    'dve|P.add("dve", lambda e: e.tensor_tensor(out=state[:, :], in0=state[:, :': 0.592,
    'dve|P.add("dve", lambda e: e.tensor_tensor(out=t1v, in0=src4, in1=cb_, op=': 0.319,
    'dve|P.add("dve", lambda e: e.tensor_tensor(out=t2v, in0=src4, in1=sb_, op=': 0.272,
    'dve|P.add("dve", lambda e: e.tensor_tensor(out=xt[xs][:, :], in0=xh_p1[s2]': 1.363,
    'dve|P.add("dve", lambda e: e.tensor_tensor(out=xt[xs][:, :], in0=xt[xs][:,': 2.19,
    'dve|P.add("dve", lambda e: e.tensor_tensor(out=zt[s2][:, :], in0=zt[s2][:,': 1.225,
    'dve|P.add(eng, lambda e, rs_=rs_, f=f: e.tensor_tensor(out=UT[:, f, :], in': 0.725,
    'pool|P.add("pool", lambda e, s2=s2: e.tensor_tensor(out=ht_p4[s2][:, :], in': 2.683,
    'pool|P.add("pool", lambda e, s2=s2: e.tensor_tensor(out=x1l[s2][:, :], in0=': 2.402,
    'pool|P.add("pool", lambda e, s2=s2: e.tensor_tensor(out=zt_p4[s2][:, :], in': 2.346,
    'pool|P.add("pool", lambda e: e.memset(V[:, :, :, 64:65], 1.0), writes=["Von': 0.666,
    'pool|P.add("pool", lambda e: e.memset(epsc[:, 0:1], EPS), writes=["epsc"])': 0.043,
    'pool|P.add("pool", lambda e: e.memset(epsc[:, 1:2], 64.0 * EPS), writes=["e': 0.098,
    'pool|P.add("pool", lambda e: e.memset(krot96[:, :, :], 0.0), writes=["krot9': 1.398,
    'pool|P.add("pool", lambda e: e.memset(ones_bf[:, :], 1.0), writes=["ones"])': 0.204,
    'pool|P.add("pool", lambda e: e.memset(ones_f[:, :], 1.0), writes=["onesf"])': 0.094,
    'pool|P.add("pool", lambda e: e.memset(pic[:, :], float(np.pi)), writes=["pi': 0.041,
    'pool|P.add("pool", lambda e: e.tensor_tensor(out=PT[pi][:, lo:lo + 128], in': 0.412,
    'pool|P.add("pool", lambda e: e.tensor_tensor(out=dst4[:, :, 0, :], in0=t1v[': 0.434,
    'pool|P.add("pool", lambda e: e.tensor_tensor(out=dst4[:, :, 1, :], in0=t1v[': 0.423,
    'pool|P.add("pool", lambda e: e.tensor_tensor(out=ht_p1[s2][:, :], in0=htmp_': 2.677,
    'pool|P.add("pool", lambda e: e.tensor_tensor(out=htmp_p1[:, :], in0=xh_p1[s': 2.793,
    'pool|P.add("pool", lambda e: e.tensor_tensor(out=kz[s2][:, :, :], in0=qkrot': 0.582,
    'pool|P.add("pool", lambda e: e.tensor_tensor(out=on[s2][:, :], in0=on[s2][:': 1.359,
    'pool|P.add("pool", lambda e: e.tensor_tensor(out=r_t[s2][:, :], in0=on[s2][': 1.422,
    'pool|P.add("pool", lambda e: e.tensor_tensor(out=state[:, :], in0=state[:, ': 1.31,
    'pool|P.add(eng, lambda e, rs_=rs_, f=f: e.tensor_tensor(out=UT[:, f, :], in': 1.102,
}
_LAST = {}
SB_END = 229376


class Op:
    __slots__ = ("eng", "fn", "reads", "writes", "dma", "waits", "signal", "cnt", "idx", "dsem", "dval", "xr", "cost", "key")

    def __init__(self, eng, fn, reads, writes, dma):
        self.eng, self.fn, self.reads, self.writes, self.dma = eng, fn, reads, writes, dma
        self.waits = []
        self.signal = False
        self.cnt = 0
        self.dsem = None
        self.dval = 0


class Prog:
    ENGS = ("pe", "act", "dve", "pool", "sp")
    ND = 40
    DPOOL = {"sp": (0, 24), "pool": (24, 16)}

    def __init__(self, nc):
        self.nc = nc
        self.ops = []
        self.last_w = {}
        self.readers = {}
        self.dma_q = {}
        self.bar_op = {}
        self.bar_from = 0

    DEF_COST = {"pe": 1.5, "act": 0.7, "dve": 0.7, "pool": 1.4, "sp": 0.05}

    def add(self, eng, fn, reads=(), writes=(), dma=False, cost=None):
        xr = tuple(k for k in reads if isinstance(k, tuple) and k and k[0] == "ps")
        writes = tuple(writes) + tuple(k for k in xr if k not in writes)
        op = Op(eng, fn, tuple(reads), tuple(writes), dma)
        op.xr = xr
        import sys as _sys, linecache as _lc
        fr = _sys._getframe(1)
        op.key = eng + "|" + _lc.getline(fr.f_code.co_filename, fr.f_lineno).strip()[:70]
        if cost is None and not dma and op.key in COST_TABLE:
            cost = COST_TABLE[op.key]
        op.cost = cost if cost is not None else ((0.65 if eng == "pool" else 0.1) if dma else self.DEF_COST[eng])
        op.idx = len(self.ops)
        deps = set()
        for k in op.reads:
            w = self.last_w.get(k)
            if w is not None:
                deps.add(w)
        for k in op.writes:
            w = self.last_w.get(k)
            if w is not None:
                deps.add(w)
            for r in self.readers.get(k, ()):
                deps.add(r)
        if eng in self.bar_op:
            deps.add(self.bar_op[eng])
        if dma:
            lst = self.dma_q.setdefault(eng, [])
            n = len(lst)
            base, cnt_ = self.DPOOL[eng]
            op.dsem = base + n % cnt_
            op.dval = 16 * (n // cnt_ + 1)
            if n >= cnt_:
                deps.add(lst[n - cnt_].idx)
            lst.append(op)
        deps.discard(op.idx)
        op.waits = sorted(deps)
        for k in op.reads:
            self.readers.setdefault(k, []).append(op.idx)
        for k in op.writes:
            self.last_w[k] = op.idx
            self.readers[k] = []
        self.ops.append(op)
        return op

    def barrier(self):
        ks = set(self.last_w.keys()) | set(self.readers.keys())
        o = self.add("sp", lambda e: e.nop(), reads=list(ks), writes=["__bar__"], cost=0.05)
        extra = set(range(self.bar_from, o.idx)) - set(o.waits)
        o.waits = sorted(set(o.waits) | extra)
        self.bar_from = o.idx
        self.bar_op["sp"] = o.idx
        for en in ("pe", "act", "dve", "pool"):
            b = self.add(en, lambda e: e.nop(), reads=["__bar__"], writes=[("__bar__", en)], cost=0.05)
            self.bar_op[en] = b.idx
        self.last_w = {"__bar__": self.last_w["__bar__"]}
        self.readers = {}

    def schedule(self):
        import heapq
        ops = self.ops
        n = len(ops)
        succ = [[] for _ in range(n)]
        indeg = [0] * n
        for op in ops:
            for j in op.waits:
                succ[j].append(op.idx)
                indeg[op.idx] += 1
        tail = [0.0] * n
        for i in range(n - 1, -1, -1):
            m = 0.0
            for j in succ[i]:
                if tail[j] > m:
                    m = tail[j]
            tail[i] = ops[i].cost + (3.0 if ops[i].dma else 0.0) + m
        ready_t = [0.0] * n
        fin_t = [0.0] * n
        eng_t = {e: 0.0 for e in self.ENGS}
        avail = {e: [] for e in self.ENGS}
        for op in ops:
            if indeg[op.idx] == 0:
                heapq.heappush(avail[op.eng], (0.0, op.idx))
        order = {e: [] for e in self.ENGS}
        done = 0
        DMA_LAT = 3.0
        while done < n:
            best = None
            for e in self.ENGS:
                h = avail[e]
                if not h:
                    continue
                t_e = eng_t[e]
                cand = None
                rdy = [x for x in h if x[0] <= t_e]
                if rdy:
                    i = (min(x[1] for x in rdy) if e in ("sp", "pool") else min(rdy, key=lambda x: (-tail[x[1]], x[1]))[1])
                    cand = (t_e, i)
                else:
                    r, i = h[0]
                    cand = (r, i)
                if best is None or cand < best[0]:
                    best = (cand, e)
            (start, i), e = best
            h = avail[e]
            for k, x in enumerate(h):
                if x[1] == i:
                    h[k] = h[-1]
                    h.pop()
                    break
            heapq.heapify(h)
            op = ops[i]
            eng_t[e] = start + op.cost
            fin_t[i] = start + op.cost + (DMA_LAT if op.dma else 0.0)
            order[e].append(op)
            done += 1
            for sidx in succ[i]:
                indeg[sidx] -= 1
                ready_t[sidx] = max(ready_t[sidx], fin_t[i])
                if indeg[sidx] == 0:
                    heapq.heappush(avail[ops[sidx].eng], (ready_t[sidx], sidx))
        self.est_total = max(fin_t) if n else 0.0
        return order

    def finalize_and_emit(self):
        self.barrier()
        nc = self.nc
        ops = self.ops
        per_eng = self.schedule()
        _LAST["per_eng"] = per_eng
        pos = {}
        for e in self.ENGS:
            for k, op in enumerate(per_eng[e]):
                pos[op.idx] = k
        need = [[] for _ in ops]
        for op in ops:
            for j in op.waits:
                p = ops[j]
                if (not p.dma) and (not op.dma) and p.eng == op.eng:
                    if op.eng == "pe":
                        continue
                    if op.eng != "pool" and not ((set(p.writes) - set(p.xr)) & set(op.reads)):
                        continue
                if p.dma and op.dma and False:
                    pass
                need[op.idx].append(j)
                p.signal = True
        for e in self.ENGS:
            c_ = 0
            for op in per_eng[e]:
                if op.dma:
                    continue
                if op.signal:
                    c_ += 1
                    op.cnt = c_
        import contextlib
        with contextlib.ExitStack() as es:
            esem = {e: es.enter_context(nc.semaphore("s_" + e)) for e in self.ENGS}
            dsem = [es.enter_context(nc.semaphore("d%d" % i)) for i in range(self.ND)]
            block = es.enter_context(nc.Block())

            def emit(eng_name, eng):
                waited = {}
                for op in per_eng[eng_name]:
                    for j in need[op.idx]:
                        p = ops[j]
                        if p.dma:
                            key, val, sem = ("d", p.dsem), p.dval, dsem[p.dsem]
                        else:
                            key, val, sem = ("e", p.eng), p.cnt, esem[p.eng]
                        if waited.get(key, 0) >= val:
                            continue
                        waited[key] = val
                        eng.wait_ge(sem, val)
                    inst = op.fn(eng)
                    if op.dma:
                        inst.then_inc(dsem[op.dsem], 16)
                    elif op.signal:
                        assert inst is not None
                        inst.then_inc(esem[op.eng], 1)

            @block.tensor
            def _(e):
                emit("pe", e)

            @block.scalar
            def _(e):
                emit("act", e)

            @block.vector
            def _(e):
                emit("dve", e)

            @block.gpsimd
            def _(e):
                emit("pool", e)

            @block.sync
            def _(e):
                emit("sp", e)


class SBAlloc:
    def __init__(self, nc, lo=SB_BASE, hi=SB_END):
        self.nc = nc
        self.cur = lo
        self.hi = hi
        self.n = 0

    def mark(self):
        return self.cur

    def reset(self, m):
        self.cur = m

    def alloc(self, shape, dtype, name=None):
        nbytes = int(np.prod(shape[1:])) * (4 if dtype in (F32, I32) else 2)
        off = (self.cur + 63) // 64 * 64
        assert off + nbytes <= self.hi, ("SBUF overflow", name, off, nbytes, self.hi)
        self.cur = off + nbytes
        self.n += 1
        return self.nc.alloc_sbuf_tensor_at("%s_%d_%d" % (name or "t", off, self.n), list(shape), dtype, offset=off)


def build(stage=99):
    nc = bass.Bass("TRN2", target_bir_lowering=False)
    P = Prog(nc)

    def din(name, shape, dt=F32):
        return nc.dram_tensor(name, list(shape), dt, kind="ExternalInput")

    x_d = din("x", [S, D])
    cT_d = din("cT", [128, 8])
    pos_d = din("pos", [128, NT], I32)
    vec_d = {n: din(n, [1, D]) for n in ("ln_in_g", "ln_in_b", "ln1_g", "ln1_b", "ln2_g", "ln2_b")}
    gng_d = din("gn_g", [1, 512])
    gnb_d = din("gn_b", [1, 512])
    wada_d = din("w_ada", [D, 6 * D])
    bada_d = din("b_ada", [1, 6 * D])
    win_d = din("w_in", [D, 1952])
    wuq_d = din("w_uq", [256, 768])
    wukv_d = din("w_ukv", [128, 1024])
    wout_d = din("w_out", [D, D])
    wff1_d = din("w_ff1", [D, DFF])
    wff2_d = din("w_ff2", [DFF, D])
    gq_d = din("gqT", [128, 2])
    gkv_d = din("gkvT", [128, 1])
    ident_d = din("ident", [128, 128])
    mask_d = din("mask01", [128, 128])
    decay_d = din("decayT", [128, 512])
    xi_d = din("xiT", [64, 512])
    zt_d = din("zT", [128, 4])
    gc_d = din("gcT", [64, 512])
    inv32_d = din("inv32", [1, 32])
    inv16_d = din("inv16", [1, 16])
    out_d = nc.dram_tensor("out", [S, D], F32, kind="ExternalOutput")
    mod_d = nc.dram_tensor("mod_scr", [1, 6 * D], F32, kind="Internal")
    x0_d = nc.dram_tensor("x0_scr", [S, D], F32, kind="Internal")
    x1_d = nc.dram_tensor("x1_scr", [S, D], F32, kind="Internal")
    w1bf_d = nc.dram_tensor("w1bf_scr", [D, DFF], BF, kind="Internal")
    w2bf_d = nc.dram_tensor("w2bf_scr", [DFF, D], BF, kind="Internal")
    dbg = {}
    if stage == 2:
        dbg["rT"] = nc.dram_tensor("dbg_rT", [128, 4, S], BF, kind="ExternalOutput")
        for nm, shp, dt_ in (("qkrot", [128, 512], BF), ("v_r", [128, 512], BF), ("sg", [128, 512], F32), ("scm", [128, 512], BF),
                             ("on", [128, 512], F32), ("r_t", [128, 512], BF), ("cs32", [128, NT * 64], F32), ("qkT", [64, 1024], BF)):
            dbg[nm] = nc.dram_tensor("dbg_" + nm, shp, dt_, kind="ExternalOutput")
    if stage == 3:
        dbg["aT"] = nc.dram_tensor("dbg_aT", [128, 4, S], BF, kind="ExternalOutput")
        dbg["KT"] = nc.dram_tensor("dbg_KT", [96, 8 * S], BF, kind="ExternalOutput")
        dbg["QT"] = nc.dram_tensor("dbg_QT", [96, 8 * S], BF, kind="ExternalOutput")
        dbg["V"] = nc.dram_tensor("dbg_V", [128, NT * 8 * 65], BF, kind="ExternalOutput")
        dbg["krot"] = nc.dram_tensor("dbg_krot", [128, NT * 96], BF, kind="ExternalOutput")
    if stage == 4:
        dbg["x1"] = nc.dram_tensor("dbg_x1", [S, D], F32, kind="ExternalOutput")

    ps = [nc.alloc_psum_tensor("ps%d" % i, [128, 512], F32) for i in range(8)]
    psb = [p.bitcast(BF) for p in ps]
    nbs = [0]

    resv = set()

    def nb():
        while True:
            nbs[0] = (nbs[0] + 1) % 8
            if nbs[0] not in resv:
                return nbs[0]

    def PK(i):
        return ("ps", i)

    sbC = SBAlloc(nc, SB_BASE, SB_BASE + 19 * 1024)
    R_H = SB_BASE + 19 * 1024
    R_R = R_H + 32 * 1024
    R_A = R_R + 16 * 1024
    R_W = R_A + 16 * 1024
    sbH = SBAlloc(nc, R_H, R_R)
    sbW = SBAlloc(nc, R_W, SB_END)

    ident = sbC.alloc([128, 128], BF, "ident")
    ones_bf = sbC.alloc([128, 128], BF, "ones")
    ones_f = sbC.alloc([128, 64], F32, "onesf")
    mask01 = sbC.alloc([128, 128], BF, "mask01")
    epsc = sbC.alloc([128, 2], F32, "epsc")
    decayT = sbC.alloc([128, 512], F32, "decayT")
    xiT = sbC.alloc([64, 512], F32, "xiT")
    zT = sbC.alloc([128, 4], F32, "zT")
    gcT = sbC.alloc([64, 512], F32, "gcT")
    cs32 = sbC.alloc([128, NT, 2, 32], F32, "cs32")
    cs16 = sbC.alloc([128, NT, 2, 16], F32, "cs16")
    gng = sbC.alloc([128, 512], F32, "gng")
    gnb = sbC.alloc([128, 512], F32, "gnb")
    pic = sbC.alloc([128, 1], F32, "pic")
    P.add("pool", lambda e: e.memset(ones_bf[:, :], 1.0), writes=["ones"])
    P.add("pool", lambda e: e.memset(ones_f[:, :], 1.0), writes=["onesf"])
    P.add("pool", lambda e: e.memset(epsc[:, 0:1], EPS), writes=["epsc"])
    P.add("pool", lambda e: e.memset(epsc[:, 1:2], 64.0 * EPS), writes=["epsc"], reads=["epsc"])
    P.add("pool", lambda e: e.memset(pic[:, :], float(np.pi)), writes=["pic"])
    P.add("pool", lambda e: e.dma_start(out=ident[:, :], in_=ident_d.ap()), writes=["ident"], dma=True)
    P.add("pool", lambda e: e.dma_start(out=mask01[:, :], in_=mask_d.ap()), writes=["mask01"], dma=True)
    for nm, t_, d_ in (("decayT", decayT, decay_d), ("xiT", xiT, xi_d), ("zT", zT, zt_d), ("gcT", gcT, gc_d)):
        P.add("sp", lambda e, t_=t_, d_=d_: e.dma_start(out=t_[:, :], in_=d_.ap()), writes=[nm], dma=True)
    P.add("sp", lambda e: e.dma_start(out=gng[:, :], in_=gng_d.ap().broadcast_to([128, 512])), writes=["gng"], dma=True)
    P.add("sp", lambda e: e.dma_start(out=gnb[:, :], in_=gnb_d.ap().broadcast_to([128, 512])), writes=["gnb"], dma=True)

    w0 = sbW.mark()
    def MODK(i):
        return [("mod_d", 2 * i), ("mod_d", 2 * i + 1)]

    def load_bcast(tile_, key, src_ap, rk=()):
        P.add("sp", lambda e: e.dma_start(out=tile_[:, :], in_=src_ap.broadcast_to([128, D])), reads=list(rk), writes=[key], dma=True)

    def mod_ap(i):
        return mod_d.ap()[:, i * D:(i + 1) * D]

    wada_v = wada_d.ap().rearrange("(kc p) n -> p kc n", p=128)

    def mod_dma(n, wa_, badc_):
        s = n % len(wa_)
        P.add("pool", lambda e: e.dma_start(out=wa_[s][:, :, :], in_=wada_v[:, :, n * 512:(n + 1) * 512]),
              writes=[("wa", s)], dma=True)
        P.add("pool", lambda e: e.dma_start(out=badc_[s][:, :], in_=bada_d.ap()[:, n * 512:(n + 1) * 512]),
              writes=[("badc", s)], dma=True)

    def mod_chunk(n, wa_, modrow_, badc_, bank):
        s = n % len(wa_)
        s2 = n % 2

        def mm(e):
            for kc in range(8):
                e.matmul(ps[bank][0:1, :], lhsT=cb[:, kc:kc + 1], rhs=wa_[s][:, kc, :], start=(kc == 0), stop=False)
            return e.matmul(ps[bank][0:1, :], lhsT=ones_bf[0:1, 0:1], rhs=badc_[s][0:1, :],
                            start=False, stop=True)
        P.add("pe", mm, reads=["cb", ("wa", s), ("badc", s), "ones"], writes=[PK(bank)], cost=4.0)
        is_sc = n in (2, 3, 8, 9)
        P.add("dve", lambda e: e.tensor_scalar(out=modrow_[s2][:, :], in0=ps[bank][0:1, :], scalar1=(1.0 if is_sc else 0.0),
                                               scalar2=None, op0=ALU.add), reads=[PK(bank)], writes=[("modrow", s2)])
        P.add("sp", lambda e: e.dma_start(out=mod_d.ap()[:, n * 512:(n + 1) * 512], in_=modrow_[s2][:, :]),
              reads=[("modrow", s2)], writes=[("mod_d", n)], dma=True)

    sbT = SBAlloc(nc, SB_END - 42 * 1024, SB_END - 2048)
    wa0 = [sbT.alloc([128, 8, 512], BF, "wa%d" % i) for i in range(4)]
    modrow0 = [sbT.alloc([1, 512], F32, "modrow%d" % i) for i in range(2)]
    badc0 = [sbT.alloc([1, 512], BF, "badc%d" % i) for i in range(4)]
    for n in range(4):
        mod_dma(n, wa0, badc0)
    g_in = sbW.alloc([128, D], F32, "g_in")
    b_in = sbW.alloc([128, D], F32, "b_in")
    G1 = sbW.alloc([128, D], F32, "G1")
    H1 = sbW.alloc([128, D], F32, "H1")
    htmp_p1 = sbW.alloc([128, D], F32, "htmp")
    load_bcast(g_in, "g_in", vec_d["ln_in_g"].ap())
    load_bcast(b_in, "b_in", vec_d["ln_in_b"].ap())
    w_in = sbW.alloc([128, 8, 1952], BF, "w_in")
    win_v = win_d.ap().rearrange("(kc p) n -> p kc n", p=128)
    for kc in range(8):
        P.add("pool", lambda e, kc=kc: e.dma_start(out=w_in[:, kc, :], in_=win_v[:, kc, :]), writes=[("w_in", kc)], dma=True)
    W_IN = [("w_in", kc) for kc in range(8)]
    gq = sbH.alloc([128, 2], F32, "gq")
    gkv = sbH.alloc([128, 1], F32, "gkv")
    P.add("sp", lambda e: e.dma_start(out=gq[:, :], in_=gq_d.ap()), writes=["gq"], dma=True)
    P.add("sp", lambda e: e.dma_start(out=gkv[:, :], in_=gkv_d.ap()), writes=["gkv"], dma=True)
    qcnT = sbH.alloc([128, 3, S], BF, "qcnT")
    krot96 = sbH.alloc([128, NT, 96], BF, "krot96")
    w_uq = sbH.alloc([128, 2, 768], BF, "w_uq")
    w_ukv = sbH.alloc([128, 1024], BF, "w_ukv")
    P.add("pool", lambda e: e.memset(krot96[:, :, :], 0.0), writes=["krot96z"])
    P.add("pool", lambda e: e.dma_start(out=w_uq[:, :, :], in_=wuq_d.ap().rearrange("(kc p) n -> p kc n", p=128)), writes=["w_uq"], dma=True)
    P.add("pool", lambda e: e.dma_start(out=w_ukv[:, :], in_=wukv_d.ap()), writes=["w_ukv"], dma=True)
    sq = [sbH.alloc([128, 512], BF, "sq%d" % i) for i in range(3)]
    rs = [sbH.alloc([128, 512], F32, "rs%d" % i) for i in range(2)]
    rT = nc.alloc_sbuf_tensor_at("rT", [128, 4, S], BF, offset=R_R)
    aT = nc.alloc_sbuf_tensor_at("aT", [128, 4, S], BF, offset=R_A)

    w1 = sbW.mark()
    pos_i = sbW.alloc([128, NT], I32, "pos_i")
    pos_f = sbW.alloc([128, NT], F32, "pos_f")
    P.add("sp", lambda e: e.dma_start(out=pos_i[:, :], in_=pos_d.ap()), writes=["pos_i"], dma=True)
    P.add("dve", lambda e: e.tensor_copy(out=pos_f[:, :], in_=pos_i[:, :]), reads=["pos_i"], writes=["pos_f"])
    TWO_PI = float(2.0 * np.pi)
    for hd, cs, inv_d in ((32, cs32, inv32_d), (16, cs16, inv16_d)):
        inv = sbW.alloc([128, hd], F32, "inv%d" % hd)
        ang = sbW.alloc([128, NT, hd], F32, "ang%d" % hd)
        rr = sbW.alloc([128, NT, hd], F32, "rr%d" % hd)
        kk = "rot%d" % hd
        P.add("sp", lambda e, inv=inv, inv_d=inv_d, hd=hd: e.dma_start(out=inv[:, :], in_=inv_d.ap().broadcast_to([128, hd])),
              writes=[kk + "inv"], dma=True)
        P.add("dve", lambda e, inv=inv, ang=ang, hd=hd: e.tensor_tensor(
            out=ang[:, :, :], in0=pos_f[:, :].unsqueeze(2).broadcast_to([128, NT, hd]),
            in1=inv[:, :].unsqueeze(1).broadcast_to([128, NT, hd]), op=ALU.mult),
            reads=["pos_f", kk + "inv"], writes=[kk + "ang"])
        qq = sbW.alloc([128, NT, hd], F32, "qq%d" % hd)
        ki = sbW.alloc([128, NT, hd], I32, "ki%d" % hd)
        C1 = 6.28125
        C2 = float(2.0 * np.pi - 6.28125)
        PI = float(np.pi)

        def sin_of(src, dst, tag, shift, ang=ang, rr=rr, qq=qq, ki=ki, kk=kk):
            RR, QQ, KI = kk + "rr", kk + "qq", kk + "ki"
            P.add("dve", lambda e: e.tensor_scalar(out=rr[:, :, :], in0=src[:, :, :], scalar1=shift, scalar2=None, op0=ALU.add),
                  reads=[kk + "ang"], writes=[RR])
            P.add("dve", lambda e: e.tensor_scalar(out=qq[:, :, :], in0=rr[:, :, :], scalar1=float(1.0 / (2.0 * np.pi)), scalar2=None, op0=ALU.mult),
                  reads=[RR], writes=[QQ])
            P.add("dve", lambda e: e.tensor_copy(out=ki[:, :, :], in_=qq[:, :, :]), reads=[QQ], writes=[KI])
            P.add("dve", lambda e: e.tensor_copy(out=qq[:, :, :], in_=ki[:, :, :]), reads=[KI], writes=[QQ])
            P.add("dve", lambda e: e.scalar_tensor_tensor(out=rr[:, :, :], in0=qq[:, :, :], scalar=-C1, in1=rr[:, :, :], op0=ALU.mult, op1=ALU.add),
                  reads=[QQ, RR], writes=[RR])
            P.add("dve", lambda e: e.scalar_tensor_tensor(out=rr[:, :, :], in0=qq[:, :, :], scalar=-C2, in1=rr[:, :, :], op0=ALU.mult, op1=ALU.add),
                  reads=[QQ, RR], writes=[RR])
            P.add("dve", lambda e: e.tensor_scalar(out=qq[:, :, :], in0=rr[:, :, :], scalar1=PI, scalar2=-2.0 * PI, op0=ALU.is_gt, op1=ALU.mult),
                  reads=[RR, QQ], writes=[QQ])
            P.add("dve", lambda e: e.tensor_tensor(out=rr[:, :, :], in0=rr[:, :, :], in1=qq[:, :, :], op=ALU.add),
                  reads=[RR, QQ], writes=[RR])
            P.add("dve", lambda e: e.tensor_scalar(out=rr[:, :, :], in0=rr[:, :, :], scalar1=PI, scalar2=-PI, op0=ALU.min, op1=ALU.max),
                  reads=[RR], writes=[RR])
            P.add("act", lambda e: e.activation(out=dst, in_=rr[:, :, :], func=AF.Sin), reads=[RR], writes=[kk + tag])
        sin_of(ang, cs[:, :, 1, :], "sin", 0.0)
        sin_of(ang, cs[:, :, 0, :], "cos", float(np.pi / 2))
    ROT32 = ["rot32sin", "rot32cos"]
    ROT16 = ["rot16sin", "rot16cos"]

    c_sb = sbW.alloc([128, 8], F32, "c_sb")
    c_act = sbW.alloc([128, 8], F32, "c_act")
    cb = sbC.alloc([128, 8], BF, "cb")
    P.add("sp", lambda e: e.dma_start(out=c_sb[:, :], in_=cT_d.ap()), writes=["c_sb"], dma=True)
    P.add("act", lambda e: e.activation(out=c_act[:, :], in_=c_sb[:, :], func=AF.Silu), reads=["c_sb"], writes=["c_act"])
    P.add("dve", lambda e: e.tensor_copy(out=cb[:, :], in_=c_act[:, :]), reads=["c_act"], writes=["cb"])
    for n in range(4):
        mod_chunk(n, wa0, modrow0, badc0, nb())

    load_bcast(H1, "H1", mod_ap(0), MODK(0))
    load_bcast(G1, "G1", mod_ap(1), MODK(1))
    P.add("dve", lambda e: e.tensor_tensor(out=htmp_p1[:, :], in0=b_in[:, :], in1=G1[:, :], op=ALU.mult),
          reads=["b_in", "G1"], writes=["htmp"])
    P.add("dve", lambda e: e.tensor_tensor(out=H1[:, :], in0=htmp_p1[:, :], in1=H1[:, :], op=ALU.add),
          reads=["htmp", "H1"], writes=["H1"])
    P.add("dve", lambda e: e.tensor_tensor(out=G1[:, :], in0=g_in[:, :], in1=G1[:, :], op=ALU.mult),
          reads=["g_in", "G1"], writes=["G1"])

    P.barrier()
    sbW.reset(w1)

    NX = 3
    xt = [sbW.alloc([128, D], F32, "xt%d" % i) for i in range(NX)]
    xh_p1 = [sbW.alloc([128, D], F32, "xh%d" % i) for i in range(2)]
    ht_p1 = [sbW.alloc([128, D], BF, "ht%d" % i) for i in range(2)]
    st_p1 = [sbW.alloc([128, 12], F32, "st%d" % i) for i in range(2)]
    mv_p1 = [sbW.alloc([128, 4], F32, "mv%d" % i) for i in range(2)]
    hT = [sbW.alloc([128, 8, 512], BF, "hT%d" % i) for i in range(2)]
    rt1 = sbW.alloc([128, 512], F32, "rt1")
    rt2 = sbW.alloc([128, 512], F32, "rt2")
    qkrot = [sbW.alloc([128, 512], BF, "qkrot%d" % i) for i in range(2)]
    v_r = [sbW.alloc([128, 512], BF, "v_r%d" % i) for i in range(2)]
    sg = [sbW.alloc([128, 512], F32, "sg%d" % i) for i in range(2)]
    qkT = [sbW.alloc([64, 8, 128], BF, "qkT%d" % i) for i in range(2)]
    qxi = [sbW.alloc([64, 4, 128], BF, "qxi%d" % i) for i in range(2)]
    kz = [sbW.alloc([128, 4, 64], BF, "kz%d" % i) for i in range(2)]
    scm = [sbW.alloc([128, 512], BF, "scm%d" % i) for i in range(2)]
    state = sbW.alloc([64, 512], F32, "state")
    state_bf = [sbW.alloc([64, 512], BF, "state_bf%d" % i) for i in range(2)]
    on = [sbW.alloc([128, 512], F32, "on%d" % i) for i in range(2)]
    r_t = [sbW.alloc([128, 512], BF, "r_t%d" % i) for i in range(2)]
    st4 = [sbW.alloc([128, 4, 6], F32, "st4%d" % i) for i in range(2)]
    mv4 = [sbW.alloc([128, 4, 2], F32, "mv4%d" % i) for i in range(2)]
    rstd4 = [sbW.alloc([128, 4], F32, "rstd4%d" % i) for i in range(2)]
    kt1 = sbW.alloc([128, 32], F32, "kt1")
    kt2 = sbW.alloc([128, 32], F32, "kt2")

    def layer_norm_stats(src, s2, key_src):
        P.add("dve", lambda e: e.bn_stats(st_p1[s2][:, 0:6], src[:, 0:512]), reads=[key_src], writes=[("st", s2)])
        P.add("dve", lambda e: e.bn_stats(st_p1[s2][:, 6:12], src[:, 512:1024]), reads=[key_src], writes=[("st", s2)])
        P.add("dve", lambda e: e.bn_aggr(mv_p1[s2][:, 0:2], st_p1[s2][:, :]), reads=[("st", s2), ("st", s2)],
              writes=[("mv", s2)])
        P.add("act", lambda e: e.activation(out=mv_p1[s2][:, 2:3], in_=mv_p1[s2][:, 1:2], func=AF.Sqrt, bias=epsc[:, 0:1], scale=1.0),
              reads=[("mv", s2), "epsc"], writes=[("mv", s2)])
        P.add("dve", lambda e: e.reciprocal(out=mv_p1[s2][:, 2:3], in_=mv_p1[s2][:, 2:3]), reads=[("mv", s2)], writes=[("mv", s2)])
        P.add("dve", lambda e: e.scalar_tensor_tensor(out=mv_p1[s2][:, 3:4], in0=mv_p1[s2][:, 0:1], scalar=-1.0,
                                                      in1=mv_p1[s2][:, 2:3], op0=ALU.mult, op1=ALU.mult),
              reads=[("mv", s2), ("mv", s2)], writes=[("mv", s2)])

    def rotary(src4, cos_ap, sin_ap, dst4, G, hd, t1, t2, rk, wk, tk):
        n = G * 2 * hd
        t1v = t1[:, 0:n].rearrange("p (g two d) -> p g two d", g=G, two=2)
        t2v = t2[:, 0:n].rearrange("p (g two d) -> p g two d", g=G, two=2)
        cb_ = cos_ap.unsqueeze(1).unsqueeze(1).broadcast_to([128, G, 2, hd])
        sb_ = sin_ap.unsqueeze(1).unsqueeze(1).broadcast_to([128, G, 2, hd])
        P.add("dve", lambda e: e.tensor_tensor(out=t1v, in0=src4, in1=cb_, op=ALU.mult), reads=list(rk), writes=[(tk, 1)])
        P.add("dve", lambda e: e.tensor_tensor(out=t2v, in0=src4, in1=sb_, op=ALU.mult), reads=list(rk), writes=[(tk, 2)])
        P.add("pool", lambda e: e.tensor_tensor(out=dst4[:, :, 0, :], in0=t1v[:, :, 0, :], in1=t2v[:, :, 1, :], op=ALU.subtract),
              reads=[(tk, 1), (tk, 2)], writes=[wk])
        P.add("pool", lambda e: e.tensor_tensor(out=dst4[:, :, 1, :], in0=t1v[:, :, 1, :], in1=t2v[:, :, 0, :], op=ALU.add),
              reads=[(tk, 1), (tk, 2), wk], writes=[wk])

    def ln_tile(t):
        xs = t % NX
        s2 = t % 2
        P.add("sp", lambda e: e.dma_start(out=xt[xs][:, :], in_=x_d.ap()[t * 128:(t + 1) * 128, :]),
              writes=[("xt", xs)], dma=True)
        layer_norm_stats(xt[xs], s2, ("xt", xs))
        P.add("act", lambda e: e.activation(out=xh_p1[s2][:, :], in_=xt[xs][:, :], func=AF.Identity,
                                            bias=mv_p1[s2][:, 3:4], scale=mv_p1[s2][:, 2:3]),
              reads=[("xt", xs), ("mv", s2), ("mv", s2)], writes=[("xh", s2)])
        P.add("dve", lambda e: e.tensor_tensor(out=xt[xs][:, :], in0=xh_p1[s2][:, :], in1=g_in[:, :], op=ALU.mult),
              reads=[("xh", s2), "g_in"], writes=[("xt", xs)])
        P.add("dve", lambda e: e.tensor_tensor(out=xt[xs][:, :], in0=xt[xs][:, :], in1=b_in[:, :], op=ALU.add),
              reads=[("xt", xs), "b_in"], writes=[("xt", xs)])
        P.add("sp", lambda e: e.dma_start(out=x0_d.ap()[t * 128:(t + 1) * 128, :], in_=xt[xs][:, :]),
              reads=[("xt", xs)], writes=[("x0_d", t)], dma=True)
        P.add("pool", lambda e: e.tensor_tensor(out=htmp_p1[:, :], in0=xh_p1[s2][:, :], in1=G1[:, :], op=ALU.mult),
              reads=[("xh", s2), "G1"], writes=["htmp"])
        P.add("pool", lambda e: e.tensor_tensor(out=ht_p1[s2][:, :], in0=htmp_p1[:, :], in1=H1[:, :], op=ALU.add),
              reads=["htmp", "H1"], writes=[("ht", s2)])

    def ln_tr(t):
        s2 = t % 2
        c, tt = divmod(t, 4)
        cs_ = c % 2
        bank = nb()

        def tr(e):
            r = None
            for kc in range(8):
                r = e.transpose(psb[bank][:, kc * 128:(kc + 1) * 128], ht_p1[s2][:, kc * 128:(kc + 1) * 128], ident[:, :])
            return r
        P.add("pe", tr, reads=[("ht", s2), "ident"], writes=[PK(bank)], cost=0.9)
        P.add("act", lambda e: e.copy(out=hT[cs_][:, :, tt * 128:(tt + 1) * 128],
                                      in_=psb[bank][:, :].rearrange("p (k t) -> p k t", k=8)),
              reads=[PK(bank)], writes=[("hT", cs_, tt)])

    def fm_chunk(c):
        cs_ = c % 2
        HT = [("hT", cs_, tt) for tt in range(4)]
        banks = [nb() for _ in range(3)]
        for mc in range(3):
            def mm(e, mc=mc):
                r = None
                for kc in range(8):
                    r = e.matmul(ps[banks[mc]][:, :], lhsT=w_in[:, kc, mc * 128:(mc + 1) * 128], rhs=hT[cs_][:, kc, :],
                                 start=(kc == 0), stop=(kc == 7))
                return r
            P.add("pe", mm, reads=HT + W_IN, writes=[PK(banks[mc])], cost=3.4)
            P.add("act", lambda e, mc=mc: e.activation(out=sq[mc][:, :], in_=ps[banks[mc]][:, :], func=AF.Square),
                  reads=[PK(banks[mc])], writes=[("sq", mc)])
        bq, bk = nb(), nb()

        def stq(e):
            e.matmul(ps[bq][:, :], lhsT=ones_bf[:, :], rhs=sq[0][:, :], start=True, stop=False)
            return e.matmul(ps[bq][:, :], lhsT=ones_bf[:, :], rhs=sq[1][:, :], start=False, stop=True)
        P.add("pe", stq, reads=[("sq", 0), ("sq", 1), "ones"], writes=[PK(bq)], cost=0.9)
        P.add("pe", lambda e: e.matmul(ps[bk][:, :], lhsT=ones_bf[:, :], rhs=sq[2][:, :], start=True, stop=True),
              reads=[("sq", 2), "ones"], writes=[PK(bk)])
        for i, (bank, n) in enumerate(((bq, 256.0), (bk, 128.0))):
            P.add("act", lambda e, i=i, bank=bank, n=n: e.activation(
                out=rs[i][:, :], in_=ps[bank][:, :], func=AF.Sqrt, bias=epsc[:, 0:1], scale=1.0 / n),
                reads=[PK(bank), "epsc"], writes=[("rs", i)])
            P.add("dve", lambda e, i=i: e.reciprocal(out=rs[i][:, :], in_=rs[i][:, :]), reads=[("rs", i)], writes=[("rs", i)])
        for mc in range(3):
            g_ap = gq[:, mc:mc + 1] if mc < 2 else gkv[:, 0:1]
            gk = "gq" if mc < 2 else "gkv"
            ri = 0 if mc < 2 else 1
            P.add("dve", lambda e, mc=mc, g_ap=g_ap, ri=ri: e.scalar_tensor_tensor(
                out=qcnT[:, mc, c * 512:(c + 1) * 512], in0=ps[banks[mc]][:, :], scalar=g_ap, in1=rs[ri][:, :],
                op0=ALU.mult, op1=ALU.mult), reads=[PK(banks[mc]), gk, ("rs", ri)], writes=[("qcnT", mc, c)])

    def S2(t, part):
        c, tt = divmod(t, 4)
        cs_ = c % 2
        s2 = t % 2
        HTK = [("hT", cs_, tt)]

        def grp(lo, n):
            bank = nb()

            def mm(e):
                r = None
                for kc in range(8):
                    r = e.matmul(ps[bank][:, 0:n], lhsT=hT[cs_][:, kc, tt * 128:(tt + 1) * 128], rhs=w_in[:, kc, lo:lo + n],
                                 start=(kc == 0), stop=(kc == 7))
                return r
            P.add("pe", mm, reads=HTK + W_IN, writes=[PK(bank)], cost=8 * max(0.06, n / 1200.0))
            return bank
        if part == 0:
            b1 = grp(416, 512)
            b4 = grp(384, 32)
            rotary(ps[b1][:, :].rearrange("p (g two d) -> p g two d", g=8, two=2),
                   cs32[:, t, 0, :], cs32[:, t, 1, :],
                   qkrot[s2][:, :].rearrange("p (g two d) -> p g two d", g=8, two=2), 8, 32, rt1, rt2,
                   [PK(b1)] + ROT32, ("qkrot", s2), "rt")
            rotary(ps[b4][:, 0:32].rearrange("p (g two d) -> p g two d", g=1, two=2),
                   cs16[:, t, 0, :], cs16[:, t, 1, :],
                   krot96[:, t, 64:96].rearrange("p (g two d) -> p g two d", g=1, two=2), 1, 16, kt1, kt2,
                   [PK(b4), "krot96z"] + ROT16, ("krot96", t), "kt")
        else:
            b2 = grp(928, 512)
            b3 = grp(1440, 512)
            P.add("act", lambda e: e.copy(out=v_r[s2][:, :], in_=ps[b2][:, :]), reads=[PK(b2)], writes=[("v_r", s2)])
            P.add("act", lambda e: e.activation(out=sg[s2][:, :], in_=ps[b3][:, :], func=AF.Silu), reads=[PK(b3)], writes=[("sg", s2)])

    def S3(t, part):
        c, tt = divmod(t, 4)
        cs_ = c % 2
        s2 = t % 2
        if part == 0:
            bt = nb()

            def trq(e):
                r = None
                for j in range(8):
                    r = e.transpose(psb[bt][0:64, j * 128:(j + 1) * 128], qkrot[s2][:, j * 64:(j + 1) * 64], ident[:, :])
                return r
            P.add("pe", trq, reads=[("qkrot", s2), "ident"], writes=[PK(bt)], cost=0.9)
            P.add("act", lambda e: e.copy(out=qkT[s2][:, :, :], in_=psb[bt][0:64, :].rearrange("p (j t) -> p j t", j=8)),
                  reads=[PK(bt)], writes=[("qkT", s2)])
            P.add("dve", lambda e: e.tensor_tensor(out=qxi[s2][:, :, :], in0=psb[bt][0:64, 0:512].rearrange("p (j t) -> p j t", j=4),
                                                   in1=xiT[:, :].rearrange("p (j t) -> p j t", j=4), op=ALU.mult),
                  reads=[PK(bt), "xiT"], writes=[("qxi", s2)])
            P.add("pool", lambda e: e.tensor_tensor(out=kz[s2][:, :, :], in0=qkrot[s2][:, 256:512].rearrange("p (h d) -> p h d", h=4),
                                                    in1=zT[:, :].unsqueeze(2).broadcast_to([128, 4, 64]), op=ALU.mult),
                  reads=[("qkrot", s2), "zT"], writes=[("kz", s2)])
        elif part == 1:
            bs = nb()

            def mm_sc(e):
                r = None
                for h in range(4):
                    r = e.matmul(ps[bs][:, h * 128:(h + 1) * 128], lhsT=qkT[s2][:, 4 + h, :], rhs=qkT[s2][:, h, :], start=True, stop=True)
                return r
            P.add("pe", mm_sc, reads=[("qkT", s2)], writes=[PK(bs)], cost=0.5)
            P.add("dve", lambda e: e.tensor_tensor(out=scm[s2][:, :], in0=ps[bs][:, :], in1=decayT[:, :], op=ALU.mult),
                  reads=[PK(bs), "decayT"], writes=[("scm", s2)])
        else:
            bo = BO[t % 2]
            sbi = t % 2

            def mm_o(e):
                r = None
                for h in range(4):
                    r = e.matmul(ps[bo][:, h * 128:(h + 1) * 128], lhsT=scm[s2][:, h * 128:(h + 1) * 128],
                                 rhs=v_r[s2][:, h * 128:(h + 1) * 128], start=True, stop=(t == 0))
                    if t > 0:
                        r = e.matmul(ps[bo][:, h * 128:(h + 1) * 128], lhsT=qxi[s2][:, h, :],
                                     rhs=state_bf[sbi][:, h * 128:(h + 1) * 128], start=False, stop=True)
                return r
            P.add("pe", mm_o, reads=[("scm", s2), ("v_r", s2), ("qxi", s2)] + ([("state_bf", sbi)] if t > 0 else []), writes=[PK(bo)], cost=1.0)
            if t < NT - 1:
                bc = nb()

                def mm_c(e):
                    r = None
                    for h in range(4):
                        r = e.matmul(ps[bc][0:64, h * 128:(h + 1) * 128], lhsT=kz[s2][:, h, :], rhs=v_r[s2][:, h * 128:(h + 1) * 128],
                                     start=True, stop=True)
                    return r
                P.add("pe", mm_c, reads=[("kz", s2), ("v_r", s2)], writes=[PK(bc)], cost=0.5)
                if t == 0:
                    P.add("dve", lambda e: e.tensor_copy(out=state[:, :], in_=ps[bc][0:64, :]), reads=[PK(bc)], writes=["state"])
                else:
                    P.add("pool", lambda e: e.tensor_tensor(out=state[:, :], in0=state[:, :], in1=gcT[:, :], op=ALU.mult),
                          reads=["state", "gcT"], writes=["state"])
                    P.add("dve", lambda e: e.tensor_tensor(out=state[:, :], in0=state[:, :], in1=ps[bc][0:64, :], op=ALU.add),
                          reads=["state", PK(bc)], writes=["state"])
                P.add("act", lambda e: e.copy(out=state_bf[1 - sbi][:, :], in_=state[:, :]), reads=["state"], writes=[("state_bf", 1 - sbi)])

    def S4a(t):
        c, tt = divmod(t, 4)
        cs_ = c % 2
        s2 = t % 2
        bo = BO[t % 2]
        for h in range(4):
            P.add("dve", lambda e, h=h: e.bn_stats(st4[s2][:, h, :], ps[bo][:, h * 128:(h + 1) * 128]), reads=[PK(bo)], writes=[("st4", s2, h)])
            P.add("dve", lambda e, h=h: e.bn_aggr(mv4[s2][:, h, :], st4[s2][:, h, :]), reads=[("st4", s2, h)], writes=[("mv4", s2, h)])
        P.add("act", lambda e: e.activation(out=rstd4[s2][:, :], in_=mv4[s2][:, :, 1], func=AF.Sqrt, bias=epsc[:, 1:2], scale=1.0),
              reads=[("mv4", s2, h) for h in range(4)] + ["epsc"], writes=[("rstd4", s2)])
        P.add("dve", lambda e: e.reciprocal(out=rstd4[s2][:, :], in_=rstd4[s2][:, :]), reads=[("rstd4", s2)], writes=[("rstd4", s2)])
        for h in range(4):
            P.add("dve", lambda e, h=h: e.tensor_scalar(out=on[s2][:, h * 128:(h + 1) * 128], in0=ps[bo][:, h * 128:(h + 1) * 128],
                                                        scalar1=mv4[s2][:, h, 0:1], scalar2=rstd4[s2][:, h:h + 1],
                                                        op0=ALU.subtract, op1=ALU.mult),
                  reads=[PK(bo), ("mv4", s2, h), ("rstd4", s2)], writes=[("on", s2, h)])
        ONK = [("on", s2, h) for h in range(4)]
        P.add("pool", lambda e: e.tensor_tensor(out=on[s2][:, :], in0=on[s2][:, :], in1=gng[:, :], op=ALU.mult),
              reads=ONK + ["gng"], writes=ONK)
        P.add("pool", lambda e: e.tensor_tensor(out=on[s2][:, :], in0=on[s2][:, :], in1=gnb[:, :], op=ALU.add),
              reads=ONK + ["gnb"], writes=ONK)
        P.add("pool", lambda e: e.tensor_tensor(out=r_t[s2][:, :], in0=on[s2][:, :], in1=sg[s2][:, :], op=ALU.mult),
              reads=ONK + [("sg", s2)], writes=[("r_t", s2)])

    def S4b(t):
        c, tt = divmod(t, 4)
        cs_ = c % 2
        s2 = t % 2
        br = nb()

        def trr(e):
            r = None
            for j in range(4):
                r = e.transpose(psb[br][:, j * 128:(j + 1) * 128], r_t[s2][:, j * 128:(j + 1) * 128], ident[:, :])
            return r
        P.add("pe", trr, reads=[("r_t", s2), "ident"], writes=[PK(br)], cost=0.5)
        P.add("act", lambda e: e.copy(out=rT[:, :, t * 128:(t + 1) * 128], in_=psb[br][:, 0:512].rearrange("p (j t) -> p j t", j=4)),
              reads=[PK(br)], writes=[("rT", t)])


    BO = (6, 7)
    resv.update(BO)
    for t in range(4):
        ln_tile(t)
        if t < 3:
            ln_tr(t)
    for step in range(NT + 2):
        if step + 3 < NT:
            ln_tr(step + 3)
        t3, t4 = step - 1, step - 2
        if 0 <= t3 < NT:
            S3(t3, 0)
        if step < NT:
            if step % 4 == 0:
                fm_chunk(step // 4)
            S2(step, 0)
        if 0 <= t3 < NT:
            S3(t3, 1)
        if 0 <= t4 < NT:
            S4a(t4)
        if step < NT:
            S2(step, 1)
        if 0 <= t3 < NT:
            S3(t3, 2)
        if step + 4 < NT:
            ln_tile(step + 4)
        if 0 <= t4 < NT:
            S4b(t4)
    resv.clear()

    fin = []
    if stage == 2:
        P.add("sp", lambda e: e.dma_start(out=dbg["rT"].ap(), in_=rT[:, :, :]), reads=[("rT", t) for t in range(NT)], writes=["dbg_rT"], dma=True)
        P.add("sp", lambda e: e.nop(), reads=["dbg_rT"], writes=["fin"])
        P.finalize_and_emit()
        return nc, dbg

    P.barrier()
    sbW.reset(w0)
    QT = sbW.alloc([96, 8, S], BF, "QT")
    KT = sbW.alloc([96, 8, S], BF, "KT")
    V = sbW.alloc([128, NT, 8, 65], BF, "V")
    PT = [sbW.alloc([128, 512], BF, "PT%d" % i) for i in range(4)]
    Qtok = [sbW.alloc([128, 8, 96], BF, "Qtok%d" % i) for i in range(2)]
    qt1 = sbW.alloc([128, 256], F32, "qt1")
    qt2 = sbW.alloc([128, 256], F32, "qt2")
    rden = [sbW.alloc([128, 512], F32, "rden%d" % i) for i in range(2)]
    o_sb = [sbW.alloc([64, 512], F32, "o_sb%d" % i) for i in range(2)]
    P.add("pool", lambda e: e.memset(V[:, :, :, 64:65], 1.0), writes=["Vones"])
    QCN = lambda mc, c: ("qcnT", mc, c)
    for c in range(4):
        for h in range(8):
            bank = nb()
            P.add("pe", lambda e, h=h, bank=bank, c=c: e.matmul(ps[bank][0:64, :], lhsT=w_ukv[:, h * 64:(h + 1) * 64],
                                                                  rhs=qcnT[:, 2, c * 512:(c + 1) * 512], start=True, stop=True),
                  reads=["w_ukv", QCN(2, c)], writes=[PK(bank)], cost=0.45)
            eng = "act" if h % 2 == 0 else "dve"
            if eng == "act":
                P.add("act", lambda e, h=h, bank=bank, c=c: e.copy(out=KT[0:64, h, c * 512:(c + 1) * 512], in_=ps[bank][0:64, :]),
                      reads=[PK(bank)], writes=[("KTn", h, c)])
            else:
                P.add("dve", lambda e, h=h, bank=bank, c=c: e.tensor_copy(out=KT[0:64, h, c * 512:(c + 1) * 512], in_=ps[bank][0:64, :]),
                      reads=[PK(bank)], writes=[("KTn", h, c)])
        for tt in range(4):
            t = c * 4 + tt
            s2 = t % 2
            bank = nb()
            P.add("pe", lambda e, t=t, bank=bank: e.matmul(ps[bank][:, :], lhsT=qcnT[:, 2, t * 128:(t + 1) * 128], rhs=w_ukv[:, 512:1024],
                                                           start=True, stop=True), reads=["w_ukv", QCN(2, c)], writes=[PK(bank)], cost=0.45)
            P.add("act", lambda e, t=t, bank=bank: e.copy(out=V[:, t, :, 0:64], in_=ps[bank][:, :].rearrange("p (h d) -> p h d", h=8)),
                  reads=[PK(bank)], writes=[("V", t)])
            for g in range(2):
                bank = nb()

                def mmq(e, t=t, bank=bank, g=g):
                    r = None
                    for k2 in range(2):
                        r = e.matmul(ps[bank][:, 0:384], lhsT=qcnT[:, k2, t * 128:(t + 1) * 128], rhs=w_uq[:, k2, g * 384:(g + 1) * 384],
                                     start=(k2 == 0), stop=(k2 == 1))
                    return r
                P.add("pe", mmq, reads=["w_uq", QCN(0, c), QCN(1, c)], writes=[PK(bank)], cost=0.7)
                qv = ps[bank][:, 0:384].rearrange("p (h d) -> p h d", h=4)
                P.add("act", lambda e, qv=qv, s2=s2, g=g: e.copy(out=Qtok[s2][:, g * 4:(g + 1) * 4, 0:64], in_=qv[:, :, 0:64]),
                      reads=[PK(bank)], writes=[("Qtok", s2, g, 0)])
                rotary(qv[:, :, 64:96].rearrange("p h (two d) -> p h two d", two=2), cs16[:, t, 0, :], cs16[:, t, 1, :],
                       Qtok[s2][:, g * 4:(g + 1) * 4, 64:96].rearrange("p h (two d) -> p h two d", two=2), 4, 16, qt1, qt2,
                       [PK(bank)] + ROT16, ("Qtok", s2, g, 1), "qt")
            bank = nb()

            def trq(e, bank=bank, s2=s2):
                r = None
                for h in range(8):
                    r = e.transpose(psb[bank][0:96, h * 128:(h + 1) * 128], Qtok[s2][:, h, :], ident[:, :])
                return r
            P.add("pe", trq, reads=[("Qtok", s2, g, i) for g in range(2) for i in range(2)] + ["ident"], writes=[PK(bank)], cost=0.9)
            P.add("act", lambda e, bank=bank, t=t: e.copy(out=QT[:, :, t * 128:(t + 1) * 128],
                                                          in_=psb[bank][0:96, :].rearrange("p (h t) -> p h t", h=8)),
                  reads=[PK(bank)], writes=[("QT", t)])
    for half in range(2):
        bank = nb()

        def trk(e, bank=bank, half=half):
            r = None
            for j in range(8):
                r = e.transpose(psb[bank][0:96, j * 128:(j + 1) * 128], krot96[:, half * 8 + j, :], ident[:, :])
            return r
        P.add("pe", trk, reads=[("krot96", half * 8 + j) for j in range(8)] + ["krot96z", "ident"], writes=[PK(bank)], cost=0.9)
        for h in range(8):
            if h % 2 == 0:
                P.add("act", lambda e, bank=bank, half=half, h=h: e.copy(out=KT[64:96, h, half * 1024:(half + 1) * 1024], in_=psb[bank][64:96, :]),
                      reads=[PK(bank)], writes=[("KTr", h, half)])
            else:
                P.add("dve", lambda e, bank=bank, half=half, h=h: e.tensor_copy(out=KT[64:96, h, half * 1024:(half + 1) * 1024], in_=psb[bank][64:96, :]),
                      reads=[PK(bank)], writes=[("KTr", h, half)])
    SC = float(96.0 ** -0.5)
    NPT = len(PT)
    steps = []
    for h in range(8):
        for c in range(4):
            for j in range(4 * c + 4):
                steps.append((h, c, j, 4 * c + 4))
    LA = 2
    SBANK, ABANK, BBANK = (0, 1, 2, 3), (4, 5), (6, 7)

    def geom(c, j):
        q0 = max(c * 512, j * 128)
        return q0, (c + 1) * 512 - q0, q0 - c * 512

    def att_qk(i):
        h, c, j, nj = steps[i]
        q0, n, lo = geom(c, j)
        bs_ = SBANK[i % 4]
        pi = i % NPT
        kreads = [("KTn", h, j // 4), ("KTr", h, j // 8)] + [("QT", tq) for tq in range(q0 // 128, (c + 1) * 4)]
        P.add("pe", lambda e: e.matmul(ps[bs_][:, lo:lo + n], lhsT=KT[:, h, j * 128:(j + 1) * 128], rhs=QT[:, h, q0:q0 + n],
                                       start=True, stop=True), reads=kreads, writes=[PK(bs_)], cost=0.3)
        P.add("act", lambda e: e.activation(out=PT[pi][:, lo:lo + n], in_=ps[bs_][:, lo:lo + n], func=AF.Exp, scale=SC),
              reads=[PK(bs_)], writes=[("PT", pi)], cost=0.7)
        if j >= 4 * c:
            P.add("pool", lambda e: e.tensor_tensor(out=PT[pi][:, lo:lo + 128], in0=PT[pi][:, lo:lo + 128], in1=mask01[:, :], op=ALU.mult),
                  reads=[("PT", pi), "mask01"], writes=[("PT", pi)], cost=0.4)

    def att_pv(i):
        h, c, j, nj = steps[i]
        q0, n, lo = geom(c, j)
        pi = i % NPT
        hc = h * 4 + c
        bo = ABANK[hc % 2]
        P.add("pe", lambda e: e.matmul(ps[bo][0:65, lo:lo + n], lhsT=V[:, j, h, :], rhs=PT[pi][:, lo:lo + n],
                                       start=(j == 0), stop=(j == nj - 1)),
              reads=[("V", j), "Vones", ("PT", pi)], writes=[PK(bo)], cost=0.3)
        if j == nj - 1:
            s2 = hc % 2
            bb = BBANK[hc % 2]
            P.add("dve", lambda e: e.reciprocal(out=rden[s2][64:65, :], in_=ps[bo][64:65, :]), reads=[PK(bo)], writes=[("rden", s2)], cost=3.6)
            P.add("act", lambda e: e.copy(out=o_sb[s2][:, :], in_=ps[bo][0:64, :]), reads=[PK(bo)], writes=[("o_sb", s2)])
            P.add("pe", lambda e: e.matmul(ps[bb][0:64, :], lhsT=ones_f[64:65, 0:64], rhs=rden[s2][64:65, :], start=True, stop=True),
                  reads=[("rden", s2), "onesf"], writes=[PK(bb)])
            po = 64 * (h % 2)
            P.add("dve", lambda e: e.tensor_tensor(out=aT[po:po + 64, h // 2, c * 512:(c + 1) * 512], in0=o_sb[s2][:, :],
                                                   in1=ps[bb][0:64, :], op=ALU.mult),
                  reads=[("o_sb", s2), PK(bb)], writes=[("aT", h, c)])

    wa3 = [sbW.alloc([128, 8, 512], BF, "wa3_%d" % i) for i in range(2)]
    modrow3 = [sbW.alloc([1, 512], F32, "modrow3_%d" % i) for i in range(2)]
    badc3 = [sbW.alloc([1, 512], BF, "badc3_%d" % i) for i in range(2)]
    mod_dma(4, wa3, badc3)
    mod_dma(5, wa3, badc3)
    nmod = [4]
    for i in range(32):
        P.add("pool", lambda e, i=i: e.dma_start(out=w2bf_d.ap()[i * 128:(i + 1) * 128, :], in_=wff2_d.ap()[i * 128:(i + 1) * 128, :]),
              writes=[("w2bf", i)], dma=True)
    for i in range(8):
        P.add("pool", lambda e, i=i: e.dma_start(out=w1bf_d.ap()[i * 128:(i + 1) * 128, :], in_=wff1_d.ap()[i * 128:(i + 1) * 128, :]),
              writes=[("w1bf", i)], dma=True)
    for i in range(len(steps) + LA):
        if i < len(steps):
            att_qk(i)
        if i >= LA:
            att_pv(i - LA)
        if i % 32 == 20 and nmod[0] < 12:
            n = nmod[0]
            nmod[0] += 1
            mod_chunk(n, wa3, modrow3, badc3, BBANK[n % 2])
            if n + 2 < 12:
                mod_dma(n + 2, wa3, badc3)
    assert nmod[0] == 12

    if stage == 3:
        P.add("sp", lambda e: e.dma_start(out=dbg["aT"].ap(), in_=aT[:, :, :]), reads=[("aT", h, c) for h in range(8) for c in range(4)],
              writes=["dbg_aT"], dma=True)
        P.add("sp", lambda e: e.dma_start(out=dbg["KT"].ap(), in_=KT[:, :, :].rearrange("p a b -> p (a b)")), writes=["dbg_KT"], dma=True)
        P.add("sp", lambda e: e.dma_start(out=dbg["QT"].ap(), in_=QT[:, :, :].rearrange("p a b -> p (a b)")), writes=["dbg_QT"], dma=True)
        P.add("sp", lambda e: e.dma_start(out=dbg["V"].ap(), in_=V[:, :, :, :].rearrange("p a b c -> p (a b c)")), writes=["dbg_V"], dma=True)
        P.add("sp", lambda e: e.dma_start(out=dbg["krot"].ap(), in_=krot96[:, :, :].rearrange("p a b -> p (a b)")), writes=["dbg_krot"], dma=True)
        P.add("sp", lambda e: e.nop(), reads=["dbg_aT"], writes=["fin"])
        P.finalize_and_emit()
        return nc, dbg

    P.barrier()
    sbW.reset(w0)
    h2T = nc.alloc_sbuf_tensor_at("h2T", [128, 8, S], BF, offset=R_H)
    w_out = sbW.alloc([128, 8, D], BF, "w_out")
    wout_v = wout_d.ap().rearrange("(kc p) n -> p kc n", p=128)
    gt1 = sbW.alloc([128, D], F32, "gt1")
    wstg = [sbW.alloc([128, D], F32, "wstg%d" % i) for i in range(2)]
    g1 = sbW.alloc([128, D], F32, "g1")
    b1 = sbW.alloc([128, D], F32, "b1")
    G2 = sbW.alloc([128, D], F32, "G2")
    H2 = sbW.alloc([128, D], F32, "H2")
    htmp_p4 = sbW.alloc([128, D], F32, "htmp4")
    load_bcast(gt1, "gt1", mod_ap(2))
    for kc in range(8):
        ws_ = kc % 2
        P.add("sp", lambda e, kc=kc, ws_=ws_: e.dma_start(out=wstg[ws_][:, :], in_=wout_v[:, kc, :]), writes=[("wstg", ws_)], dma=True)
        P.add("dve" if kc % 2 else "pool", lambda e, kc=kc, ws_=ws_: e.tensor_tensor(out=w_out[:, kc, :], in0=wstg[ws_][:, :], in1=gt1[:, :], op=ALU.mult),
              reads=[("wstg", ws_), "gt1"], writes=[("w_out", kc)])
    load_bcast(g1, "g1", vec_d["ln1_g"].ap())
    load_bcast(b1, "b1", vec_d["ln1_b"].ap())
    load_bcast(H2, "H2", mod_ap(3))
    load_bcast(G2, "G2", mod_ap(4))
    P.add("dve", lambda e: e.tensor_tensor(out=htmp_p4[:, :], in0=b1[:, :], in1=G2[:, :], op=ALU.mult), reads=["b1", "G2"], writes=["htmp"])
    P.add("dve", lambda e: e.tensor_tensor(out=H2[:, :], in0=htmp_p4[:, :], in1=H2[:, :], op=ALU.add), reads=["htmp", "H2"], writes=["H2"])
    P.add("dve", lambda e: e.tensor_tensor(out=G2[:, :], in0=g1[:, :], in1=G2[:, :], op=ALU.mult), reads=["g1", "G2"], writes=["G2"])
    NS4 = 3
    x0l = [sbW.alloc([128, D], F32, "x0l%d" % i) for i in range(NS4)]
    zt_p4 = [sbW.alloc([128, D], F32, "zt%d" % i) for i in range(NS4)]
    x1t = [sbW.alloc([128, D], F32, "x1t%d" % i) for i in range(NS4)]
    ht_p4 = [sbW.alloc([128, D], BF, "ht4%d" % i) for i in range(NS4)]
    st_p4 = [sbW.alloc([128, 12], F32, "st4_%d" % i) for i in range(NS4)]
    mv_p4 = [sbW.alloc([128, 4], F32, "mv4_%d" % i) for i in range(NS4)]
    WOUT = [("w_out", kc) for kc in range(8)]

    def resid_ln(t, s2, banks, xl, xkey, gt, gtk, zt, st, mv, aff=None):
        ZK = [("zt", s2, 0), ("zt", s2, 1)]
        if gt is None:
            for hf in range(2):
                P.add("dve", lambda e, hf=hf: e.scalar_tensor_tensor(out=zt[s2][:, hf * 512:(hf + 1) * 512], in0=xl[:, hf * 512:(hf + 1) * 512],
                                                                     scalar=ALPHA, in1=ps[banks[hf]][:, :], op0=ALU.mult, op1=ALU.add),
                      reads=[PK(banks[hf]), xkey], writes=[("zt", s2, hf)])
        else:
            for hf in range(2):
                P.add("dve", lambda e, hf=hf: e.tensor_tensor(out=zt[s2][:, hf * 512:(hf + 1) * 512], in0=ps[banks[hf]][:, :],
                                                              in1=gt[:, hf * 512:(hf + 1) * 512], op=ALU.mult),
                      reads=[PK(banks[hf]), gtk], writes=[("zt", s2, hf)])
        if gt is None:
            pass
        elif aff is None:
            P.add("dve", lambda e: e.scalar_tensor_tensor(out=zt[s2][:, :], in0=xl[:, :], scalar=ALPHA, in1=zt[s2][:, :],
                                                           op0=ALU.mult, op1=ALU.add), reads=ZK + [xkey], writes=ZK)
        else:
            Ab = aff
            P.add("dve", lambda e: e.scalar_tensor_tensor(out=zt[s2][:, :], in0=xl[:, :], scalar=ALPHA, in1=zt[s2][:, :],
                                                           op0=ALU.mult, op1=ALU.add), reads=ZK + [xkey], writes=ZK)
            P.add("dve", lambda e: e.tensor_tensor(out=zt[s2][:, :], in0=zt[s2][:, :], in1=Ab[:, :], op=ALU.add), reads=ZK + ["Ab1"], writes=ZK)
        P.add("dve", lambda e: e.bn_stats(st[s2][:, 0:6], zt[s2][:, 0:512]), reads=ZK, writes=[("st", s2)])
        P.add("dve", lambda e: e.bn_stats(st[s2][:, 6:12], zt[s2][:, 512:1024]), reads=ZK, writes=[("st", s2)])
        P.add("dve", lambda e: e.bn_aggr(mv[s2][:, 0:2], st[s2][:, :]), reads=[("st", s2), ("st", s2)], writes=[("mv", s2)])
        P.add("act", lambda e: e.activation(out=mv[s2][:, 2:3], in_=mv[s2][:, 1:2], func=AF.Sqrt, bias=epsc[:, 0:1], scale=1.0),
              reads=[("mv", s2), "epsc"], writes=[("mv", s2)])
        P.add("dve", lambda e: e.reciprocal(out=mv[s2][:, 2:3], in_=mv[s2][:, 2:3]), reads=[("mv", s2)], writes=[("mv", s2)])
        P.add("dve", lambda e: e.scalar_tensor_tensor(out=mv[s2][:, 3:4], in0=mv[s2][:, 0:1], scalar=-1.0, in1=mv[s2][:, 2:3],
                                                      op0=ALU.mult, op1=ALU.mult),
              reads=[("mv", s2), ("mv", s2)], writes=[("mv", s2)])
        return ZK

    def resid_norm(s2, zt, mv):
        ZK = [("zt", s2, 0), ("zt", s2, 1)]
        P.add("act", lambda e: e.activation(out=zt[s2][:, :], in_=zt[s2][:, :], func=AF.Identity, bias=mv[s2][:, 3:4], scale=mv[s2][:, 2:3]),
              reads=ZK + [("mv", s2), ("mv", s2)], writes=ZK)
        return ZK

    def p4_A(t):
        s2 = t % NS4
        P.add("sp", lambda e, t=t, s2=s2: e.dma_start(out=x0l[s2][:, :], in_=x0_d.ap()[t * 128:(t + 1) * 128, :]), writes=[("x0l", s2)], dma=True)
        banks = [nb(), nb()]
        for hf in range(2):
            def mmo(e, hf=hf, t=t, banks=banks):
                r = None
                for kc in range(8):
                    src = aT[:, kc, t * 128:(t + 1) * 128] if kc < 4 else rT[:, kc - 4, t * 128:(t + 1) * 128]
                    r = e.matmul(ps[banks[hf]][:, :], lhsT=src, rhs=w_out[:, kc, hf * 512:(hf + 1) * 512], start=(kc == 0), stop=(kc == 7))
                return r
            P.add("pe", mmo, reads=WOUT, writes=[PK(banks[hf])], cost=2.6)
        ZK = resid_ln(t, s2, banks, x0l[s2], ("x0l", s2), None, None, zt_p4, st_p4, mv_p4)

    def p4_B(t):
        s2 = t % NS4
        ZK = resid_norm(s2, zt_p4, mv_p4)
        P.add("dve", lambda e, s2=s2: e.tensor_tensor(out=x1t[s2][:, :], in0=zt_p4[s2][:, :], in1=g1[:, :], op=ALU.mult),
              reads=ZK + ["g1"], writes=[("x1t", s2)])
        P.add("sp", lambda e, t=t, s2=s2: e.dma_start(out=x1_d.ap()[t * 128:(t + 1) * 128, :], in_=x1t[s2][:, :]),
              reads=[("x1t", s2)], writes=[("x1_d", t)], dma=True)
        P.add("pool", lambda e, s2=s2: e.tensor_tensor(out=zt_p4[s2][:, :], in0=zt_p4[s2][:, :], in1=G2[:, :], op=ALU.mult),
              reads=ZK + ["G2"], writes=ZK)
        P.add("pool", lambda e, s2=s2: e.tensor_tensor(out=ht_p4[s2][:, :], in0=zt_p4[s2][:, :], in1=H2[:, :], op=ALU.add),
              reads=ZK + ["H2"], writes=[("ht", s2)])
        bank = nb()

        def tr(e, bank=bank, s2=s2):
            r = None
            for kc in range(8):
                r = e.transpose(psb[bank][:, kc * 128:(kc + 1) * 128], ht_p4[s2][:, kc * 128:(kc + 1) * 128], ident[:, :])
            return r
        P.add("pe", tr, reads=[("ht", s2), "ident"], writes=[PK(bank)], cost=0.9)
        P.add("act", lambda e, bank=bank, t=t: e.copy(out=h2T[:, :, t * 128:(t + 1) * 128], in_=psb[bank][:, :].rearrange("p (k t) -> p k t", k=8)),
              reads=[PK(bank)], writes=[("h2T", t)])


    p4_A(0)
    p4_A(1)
    for t in range(NT):
        if t + 2 < NT:
            p4_A(t + 2)
        p4_B(t)

    if stage == 4:
        P.add("sp", lambda e: e.dma_start(out=dbg["x1"].ap(), in_=x1_d.ap()), reads=[("x1_d", t) for t in range(NT)], writes=["dbg_x1"], dma=True)
        P.add("sp", lambda e: e.nop(), reads=["dbg_x1"], writes=["fin"])
        P.finalize_and_emit()
        return nc, dbg

    P.barrier()
    sb5 = SBAlloc(nc, R_R, SB_END)
    W2 = sb5.alloc([128, 32, D], BF, "W2")
    UT = sb5.alloc([128, 32, 512], BF, "UT")
    NW1 = 3
    W1b = [sb5.alloc([128, 8, 512], BF, "W1b%d" % i) for i in range(NW1)]
    gt2 = sb5.alloc([128, D], F32, "gt2")
    g2 = sb5.alloc([128, D], F32, "g2")
    b2 = sb5.alloc([128, D], F32, "b2")
    x1l = [sb5.alloc([128, D], F32, "x1l%d" % i) for i in range(2)]
    zt_p5 = [sb5.alloc([128, D], F32, "zt5%d" % i) for i in range(2)]
    rl = [sb5.alloc([128, 512], F32, "rl%d" % i) for i in range(2)]
    st_p5 = [sb5.alloc([128, 12], F32, "st5_%d" % i) for i in range(2)]
    mv_p5 = [sb5.alloc([128, 4], F32, "mv5_%d" % i) for i in range(2)]
    w1bf_v = w1bf_d.ap().rearrange("(kc p) n -> p kc n", p=128)

    def w1_dma(g):
        blk = g % 8
        ws = g % NW1
        P.add("sp", lambda e: e.dma_start(out=W1b[ws][:, :, :], in_=w1bf_v[:, :, blk * 512:(blk + 1) * 512]),
              reads=[("w1bf", i) for i in range(8)], writes=[("W1b", ws)], dma=True)

    def x1_dma(t):
        s2 = t % 2
        P.add("sp", lambda e: e.dma_start(out=x1l[s2][:, :], in_=x1_d.ap()[t * 128:(t + 1) * 128, :]),
              reads=[("x1_d", t)], writes=[("x1l", s2)], dma=True)

    w1_dma(0)
    w1_dma(1)
    Ab1 = sb5.alloc([128, D], F32, "Ab1")
    load_bcast(Ab1, "Ab1", vec_d["ln1_b"].ap())
    P.add("dve", lambda e: e.tensor_scalar(out=Ab1[:, :], in0=Ab1[:, :], scalar1=ALPHA, scalar2=None, op0=ALU.mult), reads=["Ab1"], writes=["Ab1"])
    load_bcast(gt2, "gt2", mod_ap(5))
    load_bcast(g2, "g2", vec_d["ln2_g"].ap())
    load_bcast(b2, "b2", vec_d["ln2_b"].ap())
    for f in range(32):
        P.add("sp", lambda e, f=f: e.dma_start(out=W2[:, f, :], in_=w2bf_d.ap()[f * 128:(f + 1) * 128, :]), reads=[("w2bf", f)], writes=[("W2", f)], dma=True)
    UTK = [("UT", f) for f in range(32)]
    W2K = [("W2", f) for f in range(32)]
    for pz in range(4):
        H2K = [("h2T", pz * 4 + i) for i in range(4)]
        x1_dma(pz * 4)
        for blk in range(8):
            g = pz * 8 + blk
            ws = g % NW1
            if g + 2 < 32:
                w1_dma(g + 2)
            for fi in range(4):
                f = blk * 4 + fi
                bank = nb()

                def mm1(e, ws=ws, fi=fi, bank=bank, pz=pz):
                    r = None
                    for kc in range(8):
                        r = e.matmul(ps[bank][:, :], lhsT=W1b[ws][:, kc, fi * 128:(fi + 1) * 128], rhs=h2T[:, kc, pz * 512:(pz + 1) * 512],
                                     start=(kc == 0), stop=(kc == 7))
                    return r
                P.add("pe", mm1, reads=[("W1b", ws)] + H2K, writes=[PK(bank)], cost=1.8)
                rs_ = f % 2
                P.add("act", lambda e, bank=bank, rs_=rs_: e.activation(out=rl[rs_][:, :], in_=ps[bank][:, :], func=AF.Relu),
                      reads=[PK(bank)], writes=[("rl", rs_)])
                eng = "pool" if f % 2 == 0 else "dve"
                P.add(eng, lambda e, rs_=rs_, f=f: e.tensor_tensor(out=UT[:, f, :], in0=rl[rs_][:, :], in1=rl[rs_][:, :], op=ALU.mult),
                      reads=[("rl", rs_)], writes=[("UT", f)])
        for tt in range(4):
            t = pz * 4 + tt
            s2 = t % 2
            if tt < 3:
                x1_dma(t + 1)
            banks = [nb(), nb()]
            for hf in range(2):
                def mm2(e, hf=hf, tt=tt, banks=banks):
                    r = None
                    for f in range(32):
                        r = e.matmul(ps[banks[hf]][:, :], lhsT=UT[:, f, tt * 128:(tt + 1) * 128], rhs=W2[:, f, hf * 512:(hf + 1) * 512],
                                     start=(f == 0), stop=(f == 31))
                    return r
                P.add("pe", mm2, reads=UTK + W2K, writes=[PK(banks[hf])], cost=7.0)
            resid_ln(t, s2, banks, x1l[s2], ("x1l", s2), gt2, "gt2", zt_p5, st_p5, mv_p5, aff=Ab1)
            ZK = resid_norm(s2, zt_p5, mv_p5)
            P.add("dve", lambda e, s2=s2: e.tensor_tensor(out=x1l[s2][:, :], in0=zt_p5[s2][:, :], in1=g2[:, :], op=ALU.mult),
                  reads=ZK + ["g2"], writes=[("x1l", s2)])
            P.add("pool", lambda e, s2=s2: e.tensor_tensor(out=x1l[s2][:, :], in0=x1l[s2][:, :], in1=b2[:, :], op=ALU.add),
                  reads=[("x1l", s2), "b2"], writes=[("x1l", s2)])
            P.add("sp", lambda e, t=t, s2=s2: e.dma_start(out=out_d.ap()[t * 128:(t + 1) * 128, :], in_=x1l[s2][:, :]),
                  reads=[("x1l", s2)], writes=[("out", t)], dma=True)
    P.add("sp", lambda e: e.nop(), reads=[("out", t) for t in range(NT)], writes=["fin"])
    P.finalize_and_emit()
    return nc, dbg


_CACHE = {}


def _consts():
    f32 = np.float32
    log_g = np.log(f32(1.0) - f32(2.0) ** (-5.0 - np.arange(4, dtype=f32))).astype(f32)
    idx = np.arange(128, dtype=f32)
    diff = idx[:, None] - idx[None, :]
    decay = np.where(diff >= 0, np.exp(log_g[:, None, None] * np.maximum(diff, 0.0)), 0.0).astype(f32)
    decayT = np.ascontiguousarray(decay.transpose(2, 0, 1)).reshape(128, 512)
    zeta = np.exp(log_g[:, None] * (127.0 - idx)).astype(f32)
    xi = np.exp(log_g[:, None] * (idx + 1.0)).astype(f32)
    gch = np.exp(log_g * 128.0).astype(f32)
    xiT = np.ascontiguousarray(np.broadcast_to(xi.reshape(1, 512), (64, 512))).astype(f32)
    gcT = np.ascontiguousarray(np.broadcast_to(np.repeat(gch, 128).reshape(1, 512), (64, 512))).astype(f32)
    zT = np.ascontiguousarray(zeta.T).astype(f32)
    inv32 = (f32(10000.0) ** (-np.arange(32, dtype=f32) / f32(32))).astype(f32).reshape(1, 32)
    inv16 = (f32(10000.0) ** (-np.arange(16, dtype=f32) / f32(16))).astype(f32).reshape(1, 16)
    k = np.arange(128)
    mask01 = (k[None, :] >= k[:, None]).astype(f32)
    return {"ident": np.eye(128, dtype=f32), "mask01": mask01, "decayT": decayT, "xiT": xiT, "zT": zT, "gcT": gcT,
            "inv32": inv32, "inv16": inv16}


def _prep_inputs(inputs):
    f = lambda a: np.ascontiguousarray(np.asarray(a, dtype=np.float32))
    wukv = f(inputs["w_ukv"][0]).reshape(128, 8, 2, 64)
    wukv_p = np.ascontiguousarray(wukv.transpose(0, 2, 1, 3)).reshape(128, 1024)
    shared = {
        "ln_in_g": f(inputs["ln_in_g"]).reshape(1, D), "ln_in_b": f(inputs["ln_in_b"]).reshape(1, D),
        "ln1_g": f(inputs["ln1_g"][0]).reshape(1, D), "ln1_b": f(inputs["ln1_b"][0]).reshape(1, D),
        "ln2_g": f(inputs["ln2_g"][0]).reshape(1, D), "ln2_b": f(inputs["ln2_b"][0]).reshape(1, D),
        "gn_g": f(inputs["ret_gn_g"][0]).reshape(1, 512), "gn_b": f(inputs["ret_gn_b"][0]).reshape(1, 512),
        "w_ada": f(inputs["w_ada"][0]), "b_ada": f(inputs["b_ada"][0]).reshape(1, 6 * D),
        "w_in": f(inputs["w_in"][0]), "w_uq": f(inputs["w_uq"][0]), "w_ukv": wukv_p,
        "w_out": f(inputs["w_out"][0]), "w_ff1": f(inputs["w_ff1"][0]), "w_ff2": f(inputs["w_ff2"][0]),
        "gqT": f(inputs["mla_q_norm"][0]).reshape(2, 128).T.copy(),
        "gkvT": f(inputs["mla_kv_norm"][0]).reshape(1, 128).T.copy(),
    }
    shared.update(_consts())
    maps = []
    for b in range(8):
        m = dict(shared)
        m["x"] = f(inputs["x"][b])
        m["cT"] = f(inputs["c"][b]).reshape(8, 128).T.copy()
        m["pos"] = np.ascontiguousarray(np.asarray(inputs["positions"][b], dtype=np.int32).reshape(NT, 128).T)
        maps.append(m)
    return maps


def run(inputs, stage=99):
    if stage not in _CACHE:
        _CACHE[stage] = build(stage)
    nc, dbg = _CACHE[stage]
    maps = _prep_inputs(inputs)
    return run_bass_kernel_spmd(nc, maps, core_ids=list(range(8)))


def kernel(**inputs):
    res = run(inputs)
    out = np.stack([np.asarray(r["out"]) for r in res.results], axis=0)
    return out.astype(np.float32)
```

```python
import numpy as np
import concourse.bass as bass
import concourse.mybir as mybir
from concourse.ap import AP
from concourse.bass_utils import run_bass_kernel_spmd

F32 = mybir.dt.float32
BF = mybir.dt.bfloat16
I32 = mybir.dt.int32
AF = mybir.ActivationFunctionType
ALU = mybir.AluOpType

S = 2048
D = 1024
NT = 16
EPS = 1e-5
ALPHA = 2.0 ** 0.25
DFF = 4096
SB_BASE = 16640
COST_TABLE = {
    'act|P.add("act", lambda e, bank=bank, half=half, h=h: e.copy(out=KT[64:96,': 0.979,
    'act|P.add("act", lambda e, bank=bank, rs_=rs_: e.activation(out=rl[rs_][:,': 0.589,
    'act|P.add("act", lambda e, bank=bank, t=t: e.copy(out=QT[:, :, t * 128:(t ': 1.007,
    'act|P.add("act", lambda e, bank=bank, t=t: e.copy(out=h2T[:, :, t * 128:(t': 0.976,
    'act|P.add("act", lambda e, h=h, bank=bank, c=c: e.copy(out=KT[0:64, h, c *': 0.534,
    'act|P.add("act", lambda e, i=i, bank=bank, n=n: e.activation(': 0.604,
    'act|P.add("act", lambda e, mc=mc: e.activation(out=sq[mc][:, :], in_=ps[ba': 0.568,
    'act|P.add("act", lambda e, qv=qv, s2=s2, g=g: e.copy(out=Qtok[s2][:, g * 4': 0.28,
    'act|P.add("act", lambda e, t=t, bank=bank: e.copy(out=V[:, t, :, 0:64], in': 0.513,
    'act|P.add("act", lambda e: e.activation(out=PT[pi][:, lo:lo + n], in_=ps[b': 0.461,
    'act|P.add("act", lambda e: e.activation(out=c_act[:, :], in_=c_sb[:, :], f': 0.209,
    'act|P.add("act", lambda e: e.activation(out=dst, in_=rr[:, :, :], func=AF.': 0.523,
    'act|P.add("act", lambda e: e.activation(out=mv[s2][:, 2:3], in_=mv[s2][:, ': 0.295,
    'act|P.add("act", lambda e: e.activation(out=mv_p1[s2][:, 2:3], in_=mv_p1[s': 0.241,
    'act|P.add("act", lambda e: e.activation(out=rstd4[s2][:, :], in_=mv4[s2][:': 0.297,
    'act|P.add("act", lambda e: e.activation(out=sg[s2][:, :], in_=ps[b3][:, :]': 0.597,
    'act|P.add("act", lambda e: e.activation(out=xh_p1[s2][:, :], in_=xt[xs][:,': 1.234,
    'act|P.add("act", lambda e: e.activation(out=zt[s2][:, :], in_=zt[s2][:, :]': 1.24,
    'act|P.add("act", lambda e: e.copy(out=hT[cs_][:, :, tt * 128:(tt + 1) * 12': 0.926,
    'act|P.add("act", lambda e: e.copy(out=o_sb[s2][:, :], in_=ps[bo][0:64, :])': 0.556,
    'act|P.add("act", lambda e: e.copy(out=qkT[s2][:, :, :], in_=psb[bt][0:64, ': 0.915,
    'act|P.add("act", lambda e: e.copy(out=rT[:, :, t * 128:(t + 1) * 128], in_': 0.51,
    'act|P.add("act", lambda e: e.copy(out=state_bf[1 - sbi][:, :], in_=state[:': 0.534,
    'act|P.add("act", lambda e: e.copy(out=v_r[s2][:, :], in_=ps[b2][:, :]), re': 0.549,
    'dve|P.add("dve", lambda e, bank=bank, half=half, h=h: e.tensor_copy(out=KT': 0.689,
    'dve|P.add("dve", lambda e, h=h, bank=bank, c=c: e.tensor_copy(out=KT[0:64,': 0.64,
    'dve|P.add("dve", lambda e, h=h: e.bn_aggr(mv4[s2][:, h, :], st4[s2][:, h, ': 0.169,
    'dve|P.add("dve", lambda e, h=h: e.bn_stats(st4[s2][:, h, :], ps[bo][:, h *': 0.205,
    'dve|P.add("dve", lambda e, h=h: e.tensor_scalar(out=on[s2][:, h * 128:(h +': 0.342,
    'dve|P.add("dve", lambda e, hf=hf: e.tensor_tensor(out=zt[s2][:, hf * 512:(': 0.66,
    'dve|P.add("dve", lambda e, i=i: e.reciprocal(out=rs[i][:, :], in_=rs[i][:,': 3.265,
    'dve|P.add("dve", lambda e, inv=inv, ang=ang, hd=hd: e.tensor_tensor(': 0.512,
    'dve|P.add("dve", lambda e, mc=mc, g_ap=g_ap, ri=ri: e.scalar_tensor_tensor': 0.651,
    'dve|P.add("dve", lambda e, s2=s2: e.tensor_tensor(out=x1l[s2][:, :], in0=z': 1.226,
    'dve|P.add("dve", lambda e, s2=s2: e.tensor_tensor(out=x1t[s2][:, :], in0=z': 1.658,
    'dve|P.add("dve", lambda e: e.bn_aggr(mv[s2][:, 0:2], st[s2][:, :]), reads=': 0.209,
    'dve|P.add("dve", lambda e: e.bn_aggr(mv_p1[s2][:, 0:2], st_p1[s2][:, :]), ': 0.174,
    'dve|P.add("dve", lambda e: e.bn_stats(st[s2][:, 0:6], zt[s2][:, 0:512]), r': 0.694,
    'dve|P.add("dve", lambda e: e.bn_stats(st[s2][:, 6:12], zt[s2][:, 512:1024]': 0.591,
    'dve|P.add("dve", lambda e: e.bn_stats(st_p1[s2][:, 0:6], src[:, 0:512]), r': 0.615,
    'dve|P.add("dve", lambda e: e.bn_stats(st_p1[s2][:, 6:12], src[:, 512:1024]': 0.599,
    'dve|P.add("dve", lambda e: e.reciprocal(out=mv[s2][:, 2:3], in_=mv[s2][:, ': 0.133,
    'dve|P.add("dve", lambda e: e.reciprocal(out=mv_p1[s2][:, 2:3], in_=mv_p1[s': 0.096,
    'dve|P.add("dve", lambda e: e.reciprocal(out=rden[s2][64:65, :], in_=ps[bo]': 3.324,
    'dve|P.add("dve", lambda e: e.reciprocal(out=rstd4[s2][:, :], in_=rstd4[s2]': 0.172,
    'dve|P.add("dve", lambda e: e.scalar_tensor_tensor(out=mv[s2][:, 3:4], in0=': 0.699,
    'dve|P.add("dve", lambda e: e.scalar_tensor_tensor(out=mv_p1[s2][:, 3:4], i': 0.793,
    'dve|P.add("dve", lambda e: e.scalar_tensor_tensor(out=rr[:, :, :], in0=qq[': 0.559,
    'dve|P.add("dve", lambda e: e.scalar_tensor_tensor(out=zt[s2][:, :], in0=xl': 1.454,
    'dve|P.add("dve", lambda e: e.tensor_copy(out=cb[:, :], in_=c_act[:, :]), r': 0.079,
    'dve|P.add("dve", lambda e: e.tensor_copy(out=ki[:, :, :], in_=qq[:, :, :])': 0.361,
    'dve|P.add("dve", lambda e: e.tensor_copy(out=pos_f[:, :], in_=pos_i[:, :])': 0.175,
    'dve|P.add("dve", lambda e: e.tensor_copy(out=qq[:, :, :], in_=ki[:, :, :])': 0.36,
    'dve|P.add("dve", lambda e: e.tensor_copy(out=state[:, :], in_=ps[bc][0:64,': 0.598,
    'dve|P.add("dve", lambda e: e.tensor_scalar(out=Ab1[:, :], in0=Ab1[:, :], s': 0.693,
    'dve|P.add("dve", lambda e: e.tensor_scalar(out=modrow_[s2][:, :], in0=ps[b': 0.671,
    'dve|P.add("dve", lambda e: e.tensor_scalar(out=qq[:, :, :], in0=rr[:, :, :': 0.349,
    'dve|P.add("dve", lambda e: e.tensor_scalar(out=rr[:, :, :], in0=rr[:, :, :': 0.36,
    'dve|P.add("dve", lambda e: e.tensor_scalar(out=rr[:, :, :], in0=src[:, :, ': 0.36,
    'dve|P.add("dve", lambda e: e.tensor_tensor(out=G1[:, :], in0=g_in[:, :], i': 1.134,
    'dve|P.add("dve", lambda e: e.tensor_tensor(out=G2[:, :], in0=g1[:, :], in1': 1.137,
    'dve|P.add("dve", lambda e: e.tensor_tensor(out=H1[:, :], in0=htmp_p1[:, :]': 1.226,
    'dve|P.add("dve", lambda e: e.tensor_tensor(out=H2[:, :], in0=htmp_p4[:, :]': 1.224,
    'dve|P.add("dve", lambda e: e.tensor_tensor(out=aT[po:po + 64, h // 2, c * ': 0.68,
    'dve|P.add("dve", lambda e: e.tensor_tensor(out=htmp_p1[:, :], in0=b_in[:, ': 1.226,
    'dve|P.add("dve", lambda e: e.tensor_tensor(out=htmp_p4[:, :], in0=b1[:, :]': 1.224,
    'dve|P.add("dve", lambda e: e.tensor_tensor(out=qxi[s2][:, :, :], in0=psb[b': 0.67,
    'dve|P.add("dve", lambda e: e.tensor_tensor(out=rr[:, :, :], in0=rr[:, :, :': 0.558,
    'dve|P.add("dve", lambda e: e.tensor_tensor(out=scm[s2][:, :], in0=ps[bs][:': 0.603,
    'dve|P.add("dve", lambda e: e.tensor_tensor(out=state[:, :], in0=state[:, :': 0.592,
    'dve|P.add("dve", lambda e: e.tensor_tensor(out=t1v, in0=src4, in1=cb_, op=': 0.319,
    'dve|P.add("dve", lambda e: e.tensor_tensor(out=t2v, in0=src4, in1=sb_, op=': 0.272,
    'dve|P.add("dve", lambda e: e.tensor_tensor(out=xt[xs][:, :], in0=xh_p1[s2]': 1.363,
    'dve|P.add("dve", lambda e: e.tensor_tensor(out=xt[xs][:, :], in0=xt[xs][:,': 2.19,
    'dve|P.add("dve", lambda e: e.tensor_tensor(out=zt[s2][:, :], in0=zt[s2][:,': 1.225,
    'dve|P.add(eng, lambda e, rs_=rs_, f=f: e.tensor_tensor(out=UT[:, f, :], in': 0.725,
    'pool|P.add("pool", lambda e, s2=s2: e.tensor_tensor(out=ht_p4[s2][:, :], in': 2.683,
    'pool|P.add("pool", lambda e, s2=s2: e.tensor_tensor(out=x1l[s2][:, :], in0=': 2.402,
    'pool|P.add("pool", lambda e, s2=s2: e.tensor_tensor(out=zt_p4[s2][:, :], in': 2.346,
    'pool|P.add("pool", lambda e: e.memset(V[:, :, :, 64:65], 1.0), writes=["Von': 0.666,
    'pool|P.add("pool", lambda e: e.memset(epsc[:, 0:1], EPS), writes=["epsc"])': 0.043,
    'pool|P.add("pool", lambda e: e.memset(epsc[:, 1:2], 64.0 * EPS), writes=["e': 0.098,
    'pool|P.add("pool", lambda e: e.memset(krot96[:, :, :], 0.0), writes=["krot9': 1.398,
    'pool|P.add("pool", lambda e: e.memset(ones_bf[:, :], 1.0), writes=["ones"])': 0.204,
    'pool|P.add("pool", lambda e: e.memset(ones_f[:, :], 1.0), writes=["onesf"])': 0.094,
    'pool|P.add("pool", lambda e: e.memset(pic[:, :], float(np.pi)), writes=["pi': 0.041,
    'pool|P.add("pool", lambda e: e.tensor_tensor(out=PT[pi][:, lo:lo + 128], in': 0.412,
    'pool|P.add("pool", lambda e: e.tensor_tensor(out=dst4[:, :, 0, :], in0=t1v[': 0.434,
    'pool|P.add("pool", lambda e: e.tensor_tensor(out=dst4[:, :, 1, :], in0=t1v[': 0.423,
    'pool|P.add("pool", lambda e: e.tensor_tensor(out=ht_p1[s2][:, :], in0=htmp_': 2.677,
    'pool|P.add("pool", lambda e: e.tensor_tensor(out=htmp_p1[:, :], in0=xh_p1[s': 2.793,
    'pool|P.add("pool", lambda e: e.tensor_tensor(out=kz[s2][:, :, :], in0=qkrot': 0.582,
    'pool|P.add("pool", lambda e: e.tensor_tensor(out=on[s2][:, :], in0=on[s2][:': 1.359,
    'pool|P.add("pool", lambda e: e.tensor_tensor(out=r_t[s2][:, :], in0=on[s2][': 1.422,
    'pool|P.add("pool", lambda e: e.tensor_tensor(out=state[:, :], in0=state[:, ': 1.31,
    'pool|P.add(eng, lambda e, rs_=rs_, f=f: e.tensor_tensor(out=UT[:, f, :], in': 1.102,
}
_LAST = {}
SB_END = 229376


class Op:
    __slots__ = ("eng", "fn", "reads", "writes", "dma", "waits", "signal", "cnt", "idx", "dsem", "dval", "xr", "cost", "key")

    def __init__(self, eng, fn, reads, writes, dma):
        self.eng, self.fn, self.reads, self.writes, self.dma = eng, fn, reads, writes, dma
        self.waits = []
        self.signal = False
        self.cnt = 0
        self.dsem = None
        self.dval = 0


class Prog:
    ENGS = ("pe", "act", "dve", "pool", "sp")
    ND = 40
    DPOOL = {"sp": (0, 24), "pool": (24, 16)}

    def __init__(self, nc):
        self.nc = nc
        self.ops = []
        self.last_w = {}
        self.readers = {}
        self.dma_q = {}
        self.bar_op = {}
        self.bar_from = 0

    DEF_COST = {"pe": 1.5, "act": 0.7, "dve": 0.7, "pool": 1.4, "sp": 0.05}

    def add(self, eng, fn, reads=(), writes=(), dma=False, cost=None):
        xr = tuple(k for k in reads if isinstance(k, tuple) and k and k[0] == "ps")
        writes = tuple(writes) + tuple(k for k in xr if k not in writes)
        op = Op(eng, fn, tuple(reads), tuple(writes), dma)
        op.xr = xr
        import sys as _sys, linecache as _lc
        fr = _sys._getframe(1)
        op.key = eng + "|" + _lc.getline(fr.f_code.co_filename, fr.f_lineno).strip()[:70]
        if cost is None and not dma and op.key in COST_TABLE:
            cost = COST_TABLE[op.key]
        op.cost = cost if cost is not None else ((1.0 if eng == "pool" else 0.1) if dma else self.DEF_COST[eng])
        op.idx = len(self.ops)
        deps = set()
        for k in op.reads:
            w = self.last_w.get(k)
            if w is not None:
                deps.add(w)
        for k in op.writes:
            w = self.last_w.get(k)
            if w is not None:
                deps.add(w)
            for r in self.readers.get(k, ()):
                deps.add(r)
        if eng in self.bar_op:
            deps.add(self.bar_op[eng])
        if dma:
            lst = self.dma_q.setdefault(eng, [])
            n = len(lst)
            base, cnt_ = self.DPOOL[eng]
            op.dsem = base + n % cnt_
            op.dval = 16 * (n // cnt_ + 1)
            if n >= cnt_:
                deps.add(lst[n - cnt_].idx)
            lst.append(op)
        deps.discard(op.idx)
        op.waits = sorted(deps)
        for k in op.reads:
            self.readers.setdefault(k, []).append(op.idx)
        for k in op.writes:
            self.last_w[k] = op.idx
            self.readers[k] = []
        self.ops.append(op)
        return op

    def barrier(self):
        ks = set(self.last_w.keys()) | set(self.readers.keys())
        o = self.add("sp", lambda e: e.nop(), reads=list(ks), writes=["__bar__"], cost=0.05)
        extra = set(range(self.bar_from, o.idx)) - set(o.waits)
        o.waits = sorted(set(o.waits) | extra)
        self.bar_from = o.idx
        self.bar_op["sp"] = o.idx
        for en in ("pe", "act", "dve", "pool"):
            b = self.add(en, lambda e: e.nop(), reads=["__bar__"], writes=[("__bar__", en)], cost=0.05)
            self.bar_op[en] = b.idx
        self.last_w = {"__bar__": self.last_w["__bar__"]}
        self.readers = {}

    def schedule(self):
        import heapq
        ops = self.ops
        n = len(ops)
        succ = [[] for _ in range(n)]
        indeg = [0] * n
        for op in ops:
            for j in op.waits:
                succ[j].append(op.idx)
                indeg[op.idx] += 1
        tail = [0.0] * n
        for i in range(n - 1, -1, -1):
            m = 0.0
            for j in succ[i]:
                if tail[j] > m:
                    m = tail[j]
            tail[i] = ops[i].cost + (3.0 if ops[i].dma else 0.0) + m
        ready_t = [0.0] * n
        fin_t = [0.0] * n
        eng_t = {e: 0.0 for e in self.ENGS}
        avail = {e: [] for e in self.ENGS}
        for op in ops:
            if indeg[op.idx] == 0:
                heapq.heappush(avail[op.eng], (0.0, op.idx))
        order = {e: [] for e in self.ENGS}
        done = 0
        DMA_LAT = 3.0
        while done < n:
            best = None
            for e in self.ENGS:
                h = avail[e]
                if not h:
                    continue
                t_e = eng_t[e]
                cand = None
                rdy = [x for x in h if x[0] <= t_e + (0.0 if e in ("sp", "pool") else 0.2)]
                if rdy:
                    i = (min(x[1] for x in rdy) if e in ("sp", "pool") else min(rdy, key=lambda x: (-tail[x[1]], x[1]))[1])
                    cand = (max(t_e, min(x[0] for x in rdy if x[1] == i)), i)
                else:
                    r, i = h[0]
                    cand = (r, i)
                if best is None or cand < best[0]:
                    best = (cand, e)
            (start, i), e = best
            h = avail[e]
            for k, x in enumerate(h):
                if x[1] == i:
                    h[k] = h[-1]
                    h.pop()
                    break
            heapq.heapify(h)
            op = ops[i]
            eng_t[e] = start + op.cost
            fin_t[i] = start + op.cost + (DMA_LAT if op.dma else 0.0)
            order[e].append(op)
            done += 1
            for sidx in succ[i]:
                indeg[sidx] -= 1
                ready_t[sidx] = max(ready_t[sidx], fin_t[i])
                if indeg[sidx] == 0:
                    heapq.heappush(avail[ops[sidx].eng], (ready_t[sidx], sidx))
        self.est_total = max(fin_t) if n else 0.0
        return order

    def finalize_and_emit(self):
        self.barrier()
        nc = self.nc
        ops = self.ops
        per_eng = self.schedule()
        _LAST["per_eng"] = per_eng
        pos = {}
        for e in self.ENGS:
            for k, op in enumerate(per_eng[e]):
                pos[op.idx] = k
        need = [[] for _ in ops]
        for op in ops:
            for j in op.waits:
                p = ops[j]
                if (not p.dma) and (not op.dma) and p.eng == op.eng:
                    if op.eng == "pe":
                        continue
                    if op.eng != "pool" and not ((set(p.writes) - set(p.xr)) & set(op.reads)):
                        continue
                if p.dma and op.dma and False:
                    pass
                need[op.idx].append(j)
                p.signal = True
        for e in self.ENGS:
            c_ = 0
            for op in per_eng[e]:
                if op.dma:
                    continue
                if op.signal:
                    c_ += 1
                    op.cnt = c_
        import contextlib
        with contextlib.ExitStack() as es:
            esem = {e: es.enter_context(nc.semaphore("s_" + e)) for e in self.ENGS}
            dsem = [es.enter_context(nc.semaphore("d%d" % i)) for i in range(self.ND)]
            block = es.enter_context(nc.Block())

            def emit(eng_name, eng):
                waited = {}
                for op in per_eng[eng_name]:
                    for j in need[op.idx]:
                        p = ops[j]
                        if p.dma:
                            key, val, sem = ("d", p.dsem), p.dval, dsem[p.dsem]
                        else:
                            key, val, sem = ("e", p.eng), p.cnt, esem[p.eng]
                        if waited.get(key, 0) >= val:
                            continue
                        waited[key] = val
                        eng.wait_ge(sem, val)
                    inst = op.fn(eng)
                    if op.dma:
                        inst.then_inc(dsem[op.dsem], 16)
                    elif op.signal:
                        assert inst is not None
                        inst.then_inc(esem[op.eng], 1)

            @block.tensor
            def _(e):
                emit("pe", e)

            @block.scalar
            def _(e):
                emit("act", e)

            @block.vector
            def _(e):
                emit("dve", e)

            @block.gpsimd
            def _(e):
                emit("pool", e)

            @block.sync
            def _(e):
                emit("sp", e)


class SBAlloc:
    def __init__(self, nc, lo=SB_BASE, hi=SB_END):
        self.nc = nc
        self.cur = lo
        self.hi = hi
        self.n = 0

    def mark(self):
        return self.cur

    def reset(self, m):
        self.cur = m

    def alloc(self, shape, dtype, name=None):
        nbytes = int(np.prod(shape[1:])) * (4 if dtype in (F32, I32) else 2)
        off = (self.cur + 63) // 64 * 64
        assert off + nbytes <= self.hi, ("SBUF overflow", name, off, nbytes, self.hi)
        self.cur = off + nbytes
        self.n += 1
        return self.nc.alloc_sbuf_tensor_at("%s_%d_%d" % (name or "t", off, self.n), list(shape), dtype, offset=off)


def build(stage=99):
    nc = bass.Bass("TRN2", target_bir_lowering=False)
    P = Prog(nc)

    def din(name, shape, dt=F32):
        return nc.dram_tensor(name, list(shape), dt, kind="ExternalInput")

    x_d = din("x", [S, D])
    cT_d = din("cT", [128, 8])
    pos_d = din("pos", [128, NT], I32)
    vec_d = {n: din(n, [1, D]) for n in ("ln_in_g", "ln_in_b", "ln1_g", "ln1_b", "ln2_g", "ln2_b")}
    gng_d = din("gn_g", [1, 512])
    gnb_d = din("gn_b", [1, 512])
    wada_d = din("w_ada", [D, 6 * D])
    bada_d = din("b_ada", [1, 6 * D])
    win_d = din("w_in", [D, 1952])
    wuq_d = din("w_uq", [256, 768])
    wukv_d = din("w_ukv", [128, 1024])
    wout_d = din("w_out", [D, D])
    wff1_d = din("w_ff1", [D, DFF])
    wff2_d = din("w_ff2", [DFF, D])
    gq_d = din("gqT", [128, 2])
    gkv_d = din("gkvT", [128, 1])
    ident_d = din("ident", [128, 128])
    mask_d = din("mask01", [128, 128])
    decay_d = din("decayT", [128, 512])
    xi_d = din("xiT", [64, 512])
    zt_d = din("zT", [128, 4])
    gc_d = din("gcT", [64, 512])
    inv32_d = din("inv32", [1, 32])
    inv16_d = din("inv16", [1, 16])
    out_d = nc.dram_tensor("out", [S, D], F32, kind="ExternalOutput")
    mod_d = nc.dram_tensor("mod_scr", [1, 6 * D], F32, kind="Internal")
    x0_d = nc.dram_tensor("x0_scr", [S, D], F32, kind="Internal")
    x1_d = nc.dram_tensor("x1_scr", [S, D], F32, kind="Internal")
    w1bf_d = nc.dram_tensor("w1bf_scr", [D, DFF], BF, kind="Internal")
    w2bf_d = nc.dram_tensor("w2bf_scr", [DFF, D], BF, kind="Internal")
    dbg = {}
    if stage == 2:
        dbg["rT"] = nc.dram_tensor("dbg_rT", [128, 4, S], BF, kind="ExternalOutput")
        for nm, shp, dt_ in (("qkrot", [128, 512], BF), ("v_r", [128, 512], BF), ("sg", [128, 512], F32), ("scm", [128, 512], BF),
                             ("on", [128, 512], F32), ("r_t", [128, 512], BF), ("cs32", [128, NT * 64], F32), ("qkT", [64, 1024], BF)):
            dbg[nm] = nc.dram_tensor("dbg_" + nm, shp, dt_, kind="ExternalOutput")
    if stage == 3:
        dbg["aT"] = nc.dram_tensor("dbg_aT", [128, 4, S], BF, kind="ExternalOutput")
        dbg["KT"] = nc.dram_tensor("dbg_KT", [96, 8 * S], BF, kind="ExternalOutput")
        dbg["QT"] = nc.dram_tensor("dbg_QT", [96, 8 * S], BF, kind="ExternalOutput")
        dbg["V"] = nc.dram_tensor("dbg_V", [128, NT * 8 * 65], BF, kind="ExternalOutput")
        dbg["krot"] = nc.dram_tensor("dbg_krot", [128, NT * 96], BF, kind="ExternalOutput")
    if stage == 4:
        dbg["x1"] = nc.dram_tensor("dbg_x1", [S, D], F32, kind="ExternalOutput")

    ps = [nc.alloc_psum_tensor("ps%d" % i, [128, 512], F32) for i in range(8)]
    psb = [p.bitcast(BF) for p in ps]
    nbs = [0]

    resv = set()

    def nb():
        while True:
            nbs[0] = (nbs[0] + 1) % 8
            if nbs[0] not in resv:
                return nbs[0]

    def PK(i):
        return ("ps", i)

    sbC = SBAlloc(nc, SB_BASE, SB_BASE + 19 * 1024)
    R_H = SB_BASE + 19 * 1024
    R_R = R_H + 32 * 1024
    R_A = R_R + 16 * 1024
    R_W = R_A + 16 * 1024
    sbH = SBAlloc(nc, R_H, R_R)
    sbW = SBAlloc(nc, R_W, SB_END)

    ident = sbC.alloc([128, 128], BF, "ident")
    ones_bf = sbC.alloc([128, 128], BF, "ones")
    ones_f = sbC.alloc([128, 64], F32, "onesf")
    mask01 = sbC.alloc([128, 128], BF, "mask01")
    epsc = sbC.alloc([128, 2], F32, "epsc")
    decayT = sbC.alloc([128, 512], F32, "decayT")
    xiT = sbC.alloc([64, 512], F32, "xiT")
    zT = sbC.alloc([128, 4], F32, "zT")
    gcT = sbC.alloc([64, 512], F32, "gcT")
    cs32 = sbC.alloc([128, NT, 2, 32], F32, "cs32")
    cs16 = sbC.alloc([128, NT, 2, 16], F32, "cs16")
    gng = sbC.alloc([128, 512], F32, "gng")
    gnb = sbC.alloc([128, 512], F32, "gnb")
    pic = sbC.alloc([128, 1], F32, "pic")
    P.add("pool", lambda e: e.memset(ones_bf[:, :], 1.0), writes=["ones"])
    P.add("pool", lambda e: e.memset(ones_f[:, :], 1.0), writes=["onesf"])
    P.add("pool", lambda e: e.memset(epsc[:, 0:1], EPS), writes=["epsc"])
    P.add("pool", lambda e: e.memset(epsc[:, 1:2], 64.0 * EPS), writes=["epsc"], reads=["epsc"])
    P.add("pool", lambda e: e.memset(pic[:, :], float(np.pi)), writes=["pic"])
    P.add("pool", lambda e: e.dma_start(out=ident[:, :], in_=ident_d.ap()), writes=["ident"], dma=True)
    P.add("pool", lambda e: e.dma_start(out=mask01[:, :], in_=mask_d.ap()), writes=["mask01"], dma=True)
    for nm, t_, d_ in (("decayT", decayT, decay_d), ("xiT", xiT, xi_d), ("zT", zT, zt_d), ("gcT", gcT, gc_d)):
        P.add("sp", lambda e, t_=t_, d_=d_: e.dma_start(out=t_[:, :], in_=d_.ap()), writes=[nm], dma=True)
    P.add("sp", lambda e: e.dma_start(out=gng[:, :], in_=gng_d.ap().broadcast_to([128, 512])), writes=["gng"], dma=True)
    P.add("sp", lambda e: e.dma_start(out=gnb[:, :], in_=gnb_d.ap().broadcast_to([128, 512])), writes=["gnb"], dma=True)

    w0 = sbW.mark()
    def MODK(i):
        return [("mod_d", 2 * i), ("mod_d", 2 * i + 1)]

    def load_bcast(tile_, key, src_ap, rk=()):
        P.add("sp", lambda e: e.dma_start(out=tile_[:, :], in_=src_ap.broadcast_to([128, D])), reads=list(rk), writes=[key], dma=True)

    def mod_ap(i):
        return mod_d.ap()[:, i * D:(i + 1) * D]

    wada_v = wada_d.ap().rearrange("(kc p) n -> p kc n", p=128)

    def mod_dma(n, wa_, badc_):
        s = n % len(wa_)
        P.add("pool", lambda e: e.dma_start(out=wa_[s][:, :, :], in_=wada_v[:, :, n * 512:(n + 1) * 512]),
              writes=[("wa", s)], dma=True)
        P.add("pool", lambda e: e.dma_start(out=badc_[s][:, :], in_=bada_d.ap()[:, n * 512:(n + 1) * 512]),
              writes=[("badc", s)], dma=True)

    def mod_chunk(n, wa_, modrow_, badc_, bank):
        s = n % len(wa_)
        s2 = n % 2

        def mm(e):
            for kc in range(8):
                e.matmul(ps[bank][0:1, :], lhsT=cb[:, kc:kc + 1], rhs=wa_[s][:, kc, :], start=(kc == 0), stop=False)
            return e.matmul(ps[bank][0:1, :], lhsT=ones_bf[0:1, 0:1], rhs=badc_[s][0:1, :],
                            start=False, stop=True)
        P.add("pe", mm, reads=["cb", ("wa", s), ("badc", s), "ones"], writes=[PK(bank)], cost=4.0)
        is_sc = n in (2, 3, 8, 9)
        P.add("dve", lambda e: e.tensor_scalar(out=modrow_[s2][:, :], in0=ps[bank][0:1, :], scalar1=(1.0 if is_sc else 0.0),
                                               scalar2=None, op0=ALU.add), reads=[PK(bank)], writes=[("modrow", s2)])
        P.add("sp", lambda e: e.dma_start(out=mod_d.ap()[:, n * 512:(n + 1) * 512], in_=modrow_[s2][:, :]),
              reads=[("modrow", s2)], writes=[("mod_d", n)], dma=True)

    sbT = SBAlloc(nc, SB_END - 42 * 1024, SB_END - 2048)
    wa0 = [sbT.alloc([128, 8, 512], BF, "wa%d" % i) for i in range(4)]
    modrow0 = [sbT.alloc([1, 512], F32, "modrow%d" % i) for i in range(2)]
    badc0 = [sbT.alloc([1, 512], BF, "badc%d" % i) for i in range(4)]
    for n in range(4):
        mod_dma(n, wa0, badc0)
    g_in = sbW.alloc([128, D], F32, "g_in")
    b_in = sbW.alloc([128, D], F32, "b_in")
    G1 = sbW.alloc([128, D], F32, "G1")
    H1 = sbW.alloc([128, D], F32, "H1")
    htmp_p1 = sbW.alloc([128, D], F32, "htmp")
    load_bcast(g_in, "g_in", vec_d["ln_in_g"].ap())
    load_bcast(b_in, "b_in", vec_d["ln_in_b"].ap())
    w_in = sbW.alloc([128, 8, 1952], BF, "w_in")
    win_v = win_d.ap().rearrange("(kc p) n -> p kc n", p=128)
    for kc in range(8):
        P.add("pool", lambda e, kc=kc: e.dma_start(out=w_in[:, kc, :], in_=win_v[:, kc, :]), writes=[("w_in", kc)], dma=True)
    W_IN = [("w_in", kc) for kc in range(8)]
    gq = sbH.alloc([128, 2], F32, "gq")
    gkv = sbH.alloc([128, 1], F32, "gkv")
    P.add("sp", lambda e: e.dma_start(out=gq[:, :], in_=gq_d.ap()), writes=["gq"], dma=True)
    P.add("sp", lambda e: e.dma_start(out=gkv[:, :], in_=gkv_d.ap()), writes=["gkv"], dma=True)
    qcnT = sbH.alloc([128, 3, S], BF, "qcnT")
    krot96 = sbH.alloc([128, NT, 96], BF, "krot96")
    w_uq = sbH.alloc([128, 2, 768], BF, "w_uq")
    w_ukv = sbH.alloc([128, 1024], BF, "w_ukv")
    P.add("pool", lambda e: e.memset(krot96[:, :, :], 0.0), writes=["krot96z"])
    P.add("pool", lambda e: e.dma_start(out=w_uq[:, :, :], in_=wuq_d.ap().rearrange("(kc p) n -> p kc n", p=128)), writes=["w_uq"], dma=True)
    P.add("pool", lambda e: e.dma_start(out=w_ukv[:, :], in_=wukv_d.ap()), writes=["w_ukv"], dma=True)
    sq = [sbH.alloc([128, 512], BF, "sq%d" % i) for i in range(3)]
    rs = [sbH.alloc([128, 512], F32, "rs%d" % i) for i in range(2)]
    rT = nc.alloc_sbuf_tensor_at("rT", [128, 4, S], BF, offset=R_R)
    aT = nc.alloc_sbuf_tensor_at("aT", [128, 4, S], BF, offset=R_A)

    w1 = sbW.mark()
    pos_i = sbW.alloc([128, NT], I32, "pos_i")
    pos_f = sbW.alloc([128, NT], F32, "pos_f")
    P.add("sp", lambda e: e.dma_start(out=pos_i[:, :], in_=pos_d.ap()), writes=["pos_i"], dma=True)
    P.add("dve", lambda e: e.tensor_copy(out=pos_f[:, :], in_=pos_i[:, :]), reads=["pos_i"], writes=["pos_f"])
    TWO_PI = float(2.0 * np.pi)
    for hd, cs, inv_d in ((32, cs32, inv32_d), (16, cs16, inv16_d)):
        inv = sbW.alloc([128, hd], F32, "inv%d" % hd)
        ang = sbW.alloc([128, NT, hd], F32, "ang%d" % hd)
        rr = sbW.alloc([128, NT, hd], F32, "rr%d" % hd)
        kk = "rot%d" % hd
        P.add("sp", lambda e, inv=inv, inv_d=inv_d, hd=hd: e.dma_start(out=inv[:, :], in_=inv_d.ap().broadcast_to([128, hd])),
              writes=[kk + "inv"], dma=True)
        P.add("dve", lambda e, inv=inv, ang=ang, hd=hd: e.tensor_tensor(
            out=ang[:, :, :], in0=pos_f[:, :].unsqueeze(2).broadcast_to([128, NT, hd]),
            in1=inv[:, :].unsqueeze(1).broadcast_to([128, NT, hd]), op=ALU.mult),
            reads=["pos_f", kk + "inv"], writes=[kk + "ang"])
        qq = sbW.alloc([128, NT, hd], F32, "qq%d" % hd)
        ki = sbW.alloc([128, NT, hd], I32, "ki%d" % hd)
        C1 = 6.28125
        C2 = float(2.0 * np.pi - 6.28125)
        PI = float(np.pi)

        def sin_of(src, dst, tag, shift, ang=ang, rr=rr, qq=qq, ki=ki, kk=kk):
            RR, QQ, KI = kk + "rr", kk + "qq", kk + "ki"
            P.add("dve", lambda e: e.tensor_scalar(out=rr[:, :, :], in0=src[:, :, :], scalar1=shift, scalar2=None, op0=ALU.add),
                  reads=[kk + "ang"], writes=[RR])
            P.add("dve", lambda e: e.tensor_scalar(out=qq[:, :, :], in0=rr[:, :, :], scalar1=float(1.0 / (2.0 * np.pi)), scalar2=None, op0=ALU.mult),
                  reads=[RR], writes=[QQ])
            P.add("dve", lambda e: e.tensor_copy(out=ki[:, :, :], in_=qq[:, :, :]), reads=[QQ], writes=[KI])
            P.add("dve", lambda e: e.tensor_copy(out=qq[:, :, :], in_=ki[:, :, :]), reads=[KI], writes=[QQ])
            P.add("dve", lambda e: e.scalar_tensor_tensor(out=rr[:, :, :], in0=qq[:, :, :], scalar=-C1, in1=rr[:, :, :], op0=ALU.mult, op1=ALU.add),
                  reads=[QQ, RR], writes=[RR])
            P.add("dve", lambda e: e.scalar_tensor_tensor(out=rr[:, :, :], in0=qq[:, :, :], scalar=-C2, in1=rr[:, :, :], op0=ALU.mult, op1=ALU.add),
                  reads=[QQ, RR], writes=[RR])
            P.add("dve", lambda e: e.tensor_scalar(out=qq[:, :, :], in0=rr[:, :, :], scalar1=PI, scalar2=-2.0 * PI, op0=ALU.is_gt, op1=ALU.mult),
                  reads=[RR, QQ], writes=[QQ])
            P.add("dve", lambda e: e.tensor_tensor(out=rr[:, :, :], in0=rr[:, :, :], in1=qq[:, :, :], op=ALU.add),
                  reads=[RR, QQ], writes=[RR])
            P.add("dve", lambda e: e.tensor_scalar(out=rr[:, :, :], in0=rr[:, :, :], scalar1=PI, scalar2=-PI, op0=ALU.min, op1=ALU.max),
                  reads=[RR], writes=[RR])
            P.add("act", lambda e: e.activation(out=dst, in_=rr[:, :, :], func=AF.Sin), reads=[RR], writes=[kk + tag])
        sin_of(ang, cs[:, :, 1, :], "sin", 0.0)
        sin_of(ang, cs[:, :, 0, :], "cos", float(np.pi / 2))
    ROT32 = ["rot32sin", "rot32cos"]
    ROT16 = ["rot16sin", "rot16cos"]

    c_sb = sbW.alloc([128, 8], F32, "c_sb")
    c_act = sbW.alloc([128, 8], F32, "c_act")
    cb = sbC.alloc([128, 8], BF, "cb")
    P.add("sp", lambda e: e.dma_start(out=c_sb[:, :], in_=cT_d.ap()), writes=["c_sb"], dma=True)
    P.add("act", lambda e: e.activation(out=c_act[:, :], in_=c_sb[:, :], func=AF.Silu), reads=["c_sb"], writes=["c_act"])
    P.add("dve", lambda e: e.tensor_copy(out=cb[:, :], in_=c_act[:, :]), reads=["c_act"], writes=["cb"])
    for n in range(4):
        mod_chunk(n, wa0, modrow0, badc0, nb())

    load_bcast(H1, "H1", mod_ap(0), MODK(0))
    load_bcast(G1, "G1", mod_ap(1), MODK(1))
    P.add("dve", lambda e: e.tensor_tensor(out=htmp_p1[:, :], in0=b_in[:, :], in1=G1[:, :], op=ALU.mult),
          reads=["b_in", "G1"], writes=["htmp"])
    P.add("dve", lambda e: e.tensor_tensor(out=H1[:, :], in0=htmp_p1[:, :], in1=H1[:, :], op=ALU.add),
          reads=["htmp", "H1"], writes=["H1"])
    P.add("dve", lambda e: e.tensor_tensor(out=G1[:, :], in0=g_in[:, :], in1=G1[:, :], op=ALU.mult),
          reads=["g_in", "G1"], writes=["G1"])

    P.barrier()
    sbW.reset(w1)

    NX = 3
    xt = [sbW.alloc([128, D], F32, "xt%d" % i) for i in range(NX)]
    xh_p1 = [sbW.alloc([128, D], F32, "xh%d" % i) for i in range(2)]
    ht_p1 = [sbW.alloc([128, D], BF, "ht%d" % i) for i in range(2)]
    st_p1 = [sbW.alloc([128, 12], F32, "st%d" % i) for i in range(2)]
    mv_p1 = [sbW.alloc([128, 4], F32, "mv%d" % i) for i in range(2)]
    hT = [sbW.alloc([128, 8, 512], BF, "hT%d" % i) for i in range(2)]
    rt1 = sbW.alloc([128, 512], F32, "rt1")
    rt2 = sbW.alloc([128, 512], F32, "rt2")
    qkrot = [sbW.alloc([128, 512], BF, "qkrot%d" % i) for i in range(2)]
    v_r = [sbW.alloc([128, 512], BF, "v_r%d" % i) for i in range(2)]
    sg = [sbW.alloc([128, 512], F32, "sg%d" % i) for i in range(2)]
    qkT = [sbW.alloc([64, 8, 128], BF, "qkT%d" % i) for i in range(2)]
    qxi = [sbW.alloc([64, 4, 128], BF, "qxi%d" % i) for i in range(2)]
    kz = [sbW.alloc([128, 4, 64], BF, "kz%d" % i) for i in range(2)]
    scm = [sbW.alloc([128, 512], BF, "scm%d" % i) for i in range(2)]
    state = sbW.alloc([64, 512], F32, "state")
    state_bf = [sbW.alloc([64, 512], BF, "state_bf%d" % i) for i in range(2)]
    on = [sbW.alloc([128, 512], F32, "on%d" % i) for i in range(2)]
    r_t = [sbW.alloc([128, 512], BF, "r_t%d" % i) for i in range(2)]
    st4 = [sbW.alloc([128, 4, 6], F32, "st4%d" % i) for i in range(2)]
    mv4 = [sbW.alloc([128, 4, 2], F32, "mv4%d" % i) for i in range(2)]
    rstd4 = [sbW.alloc([128, 4], F32, "rstd4%d" % i) for i in range(2)]
    kt1 = sbW.alloc([128, 32], F32, "kt1")
    kt2 = sbW.alloc([128, 32], F32, "kt2")

    def layer_norm_stats(src, s2, key_src):
        P.add("dve", lambda e: e.bn_stats(st_p1[s2][:, 0:6], src[:, 0:512]), reads=[key_src], writes=[("st", s2)])
        P.add("dve", lambda e: e.bn_stats(st_p1[s2][:, 6:12], src[:, 512:1024]), reads=[key_src], writes=[("st", s2)])
        P.add("dve", lambda e: e.bn_aggr(mv_p1[s2][:, 0:2], st_p1[s2][:, :]), reads=[("st", s2), ("st", s2)],
              writes=[("mv", s2)])
        P.add("act", lambda e: e.activation(out=mv_p1[s2][:, 2:3], in_=mv_p1[s2][:, 1:2], func=AF.Sqrt, bias=epsc[:, 0:1], scale=1.0),
              reads=[("mv", s2), "epsc"], writes=[("mv", s2)])
        P.add("dve", lambda e: e.reciprocal(out=mv_p1[s2][:, 2:3], in_=mv_p1[s2][:, 2:3]), reads=[("mv", s2)], writes=[("mv", s2)])
        P.add("dve", lambda e: e.scalar_tensor_tensor(out=mv_p1[s2][:, 3:4], in0=mv_p1[s2][:, 0:1], scalar=-1.0,
                                                      in1=mv_p1[s2][:, 2:3], op0=ALU.mult, op1=ALU.mult),
              reads=[("mv", s2), ("mv", s2)], writes=[("mv", s2)])

    def rotary(src4, cos_ap, sin_ap, dst4, G, hd, t1, t2, rk, wk, tk):
        n = G * 2 * hd
        t1v = t1[:, 0:n].rearrange("p (g two d) -> p g two d", g=G, two=2)
        t2v = t2[:, 0:n].rearrange("p (g two d) -> p g two d", g=G, two=2)
        cb_ = cos_ap.unsqueeze(1).unsqueeze(1).broadcast_to([128, G, 2, hd])
        sb_ = sin_ap.unsqueeze(1).unsqueeze(1).broadcast_to([128, G, 2, hd])
        P.add("dve", lambda e: e.tensor_tensor(out=t1v, in0=src4, in1=cb_, op=ALU.mult), reads=list(rk), writes=[(tk, 1)])
        P.add("dve", lambda e: e.tensor_tensor(out=t2v, in0=src4, in1=sb_, op=ALU.mult), reads=list(rk), writes=[(tk, 2)])
        P.add("pool", lambda e: e.tensor_tensor(out=dst4[:, :, 0, :], in0=t1v[:, :, 0, :], in1=t2v[:, :, 1, :], op=ALU.subtract),
              reads=[(tk, 1), (tk, 2)], writes=[wk])
        P.add("pool", lambda e: e.tensor_tensor(out=dst4[:, :, 1, :], in0=t1v[:, :, 1, :], in1=t2v[:, :, 0, :], op=ALU.add),
              reads=[(tk, 1), (tk, 2), wk], writes=[wk])

    def ln_tile(t):
        xs = t % NX
        s2 = t % 2
        P.add("sp", lambda e: e.dma_start(out=xt[xs][:, :], in_=x_d.ap()[t * 128:(t + 1) * 128, :]),
              writes=[("xt", xs)], dma=True)
        layer_norm_stats(xt[xs], s2, ("xt", xs))
        P.add("act", lambda e: e.activation(out=xh_p1[s2][:, :], in_=xt[xs][:, :], func=AF.Identity,
                                            bias=mv_p1[s2][:, 3:4], scale=mv_p1[s2][:, 2:3]),
              reads=[("xt", xs), ("mv", s2), ("mv", s2)], writes=[("xh", s2)])
        P.add("dve", lambda e: e.tensor_tensor(out=xt[xs][:, :], in0=xh_p1[s2][:, :], in1=g_in[:, :], op=ALU.mult),
              reads=[("xh", s2), "g_in"], writes=[("xt", xs)])
        P.add("dve", lambda e: e.tensor_tensor(out=xt[xs][:, :], in0=xt[xs][:, :], in1=b_in[:, :], op=ALU.add),
              reads=[("xt", xs), "b_in"], writes=[("xt", xs)])
        P.add("sp", lambda e: e.dma_start(out=x0_d.ap()[t * 128:(t + 1) * 128, :], in_=xt[xs][:, :]),
              reads=[("xt", xs)], writes=[("x0_d", t)], dma=True)
        P.add("pool", lambda e: e.tensor_tensor(out=htmp_p1[:, :], in0=xh_p1[s2][:, :], in1=G1[:, :], op=ALU.mult),
              reads=[("xh", s2), "G1"], writes=["htmp"])
        P.add("pool", lambda e: e.tensor_tensor(out=ht_p1[s2][:, :], in0=htmp_p1[:, :], in1=H1[:, :], op=ALU.add),
              reads=["htmp", "H1"], writes=[("ht", s2)])

    def ln_tr(t):
        s2 = t % 2
        c, tt = divmod(t, 4)
        cs_ = c % 2
        bank = nb()

        def tr(e):
            r = None
            for kc in range(8):
                r = e.transpose(psb[bank][:, kc * 128:(kc + 1) * 128], ht_p1[s2][:, kc * 128:(kc + 1) * 128], ident[:, :])
            return r
        P.add("pe", tr, reads=[("ht", s2), "ident"], writes=[PK(bank)], cost=0.9)
        P.add("act", lambda e: e.copy(out=hT[cs_][:, :, tt * 128:(tt + 1) * 128],
                                      in_=psb[bank][:, :].rearrange("p (k t) -> p k t", k=8)),
              reads=[PK(bank)], writes=[("hT", cs_, tt)])

    def fm_chunk(c):
        cs_ = c % 2
        HT = [("hT", cs_, tt) for tt in range(4)]
        banks = [nb() for _ in range(3)]
        for mc in range(3):
            def mm(e, mc=mc):
                r = None
                for kc in range(8):
                    r = e.matmul(ps[banks[mc]][:, :], lhsT=w_in[:, kc, mc * 128:(mc + 1) * 128], rhs=hT[cs_][:, kc, :],
                                 start=(kc == 0), stop=(kc == 7))
                return r
            P.add("pe", mm, reads=HT + W_IN, writes=[PK(banks[mc])], cost=3.4)
            P.add("act", lambda e, mc=mc: e.activation(out=sq[mc][:, :], in_=ps[banks[mc]][:, :], func=AF.Square),
                  reads=[PK(banks[mc])], writes=[("sq", mc)])
        bq, bk = nb(), nb()

        def stq(e):
            e.matmul(ps[bq][:, :], lhsT=ones_bf[:, :], rhs=sq[0][:, :], start=True, stop=False)
            return e.matmul(ps[bq][:, :], lhsT=ones_bf[:, :], rhs=sq[1][:, :], start=False, stop=True)
        P.add("pe", stq, reads=[("sq", 0), ("sq", 1), "ones"], writes=[PK(bq)], cost=0.9)
        P.add("pe", lambda e: e.matmul(ps[bk][:, :], lhsT=ones_bf[:, :], rhs=sq[2][:, :], start=True, stop=True),
              reads=[("sq", 2), "ones"], writes=[PK(bk)])
        for i, (bank, n) in enumerate(((bq, 256.0), (bk, 128.0))):
            P.add("act", lambda e, i=i, bank=bank, n=n: e.activation(
                out=rs[i][:, :], in_=ps[bank][:, :], func=AF.Sqrt, bias=epsc[:, 0:1], scale=1.0 / n),
                reads=[PK(bank), "epsc"], writes=[("rs", i)])
            P.add("dve", lambda e, i=i: e.reciprocal(out=rs[i][:, :], in_=rs[i][:, :]), reads=[("rs", i)], writes=[("rs", i)])
        for mc in range(3):
            g_ap = gq[:, mc:mc + 1] if mc < 2 else gkv[:, 0:1]
            gk = "gq" if mc < 2 else "gkv"
            ri = 0 if mc < 2 else 1
            P.add("dve", lambda e, mc=mc, g_ap=g_ap, ri=ri: e.scalar_tensor_tensor(
                out=qcnT[:, mc, c * 512:(c + 1) * 512], in0=ps[banks[mc]][:, :], scalar=g_ap, in1=rs[ri][:, :],
                op0=ALU.mult, op1=ALU.mult), reads=[PK(banks[mc]), gk, ("rs", ri)], writes=[("qcnT", mc, c)])

    def S2(t, part):
        c, tt = divmod(t, 4)
        cs_ = c % 2
        s2 = t % 2
        HTK = [("hT", cs_, tt)]

        def grp(lo, n):
            bank = nb()

            def mm(e):
                r = None
                for kc in range(8):
                    r = e.matmul(ps[bank][:, 0:n], lhsT=hT[cs_][:, kc, tt * 128:(tt + 1) * 128], rhs=w_in[:, kc, lo:lo + n],
                                 start=(kc == 0), stop=(kc == 7))
                return r
            P.add("pe", mm, reads=HTK + W_IN, writes=[PK(bank)], cost=8 * max(0.06, n / 1200.0))
            return bank
        if part == 0:
            b1 = grp(416, 512)
            b4 = grp(384, 32)
            rotary(ps[b1][:, :].rearrange("p (g two d) -> p g two d", g=8, two=2),
                   cs32[:, t, 0, :], cs32[:, t, 1, :],
                   qkrot[s2][:, :].rearrange("p (g two d) -> p g two d", g=8, two=2), 8, 32, rt1, rt2,
                   [PK(b1)] + ROT32, ("qkrot", s2), "rt")
            rotary(ps[b4][:, 0:32].rearrange("p (g two d) -> p g two d", g=1, two=2),
                   cs16[:, t, 0, :], cs16[:, t, 1, :],
                   krot96[:, t, 64:96].rearrange("p (g two d) -> p g two d", g=1, two=2), 1, 16, kt1, kt2,
                   [PK(b4), "krot96z"] + ROT16, ("krot96", t), "kt")
        else:
            b2 = grp(928, 512)
            b3 = grp(1440, 512)
            P.add("act", lambda e: e.copy(out=v_r[s2][:, :], in_=ps[b2][:, :]), reads=[PK(b2)], writes=[("v_r", s2)])
            P.add("act", lambda e: e.activation(out=sg[s2][:, :], in_=ps[b3][:, :], func=AF.Silu), reads=[PK(b3)], writes=[("sg", s2)])

    def S3(t, part):
        c, tt = divmod(t, 4)
        cs_ = c % 2
        s2 = t % 2
        if part == 0:
            bt = nb()

            def trq(e):
                r = None
                for j in range(8):
                    r = e.transpose(psb[bt][0:64, j * 128:(j + 1) * 128], qkrot[s2][:, j * 64:(j + 1) * 64], ident[:, :])
                return r
            P.add("pe", trq, reads=[("qkrot", s2), "ident"], writes=[PK(bt)], cost=0.9)
            P.add("act", lambda e: e.copy(out=qkT[s2][:, :, :], in_=psb[bt][0:64, :].rearrange("p (j t) -> p j t", j=8)),
                  reads=[PK(bt)], writes=[("qkT", s2)])
            P.add("dve", lambda e: e.tensor_tensor(out=qxi[s2][:, :, :], in0=psb[bt][0:64, 0:512].rearrange("p (j t) -> p j t", j=4),
                                                   in1=xiT[:, :].rearrange("p (j t) -> p j t", j=4), op=ALU.mult),
                  reads=[PK(bt), "xiT"], writes=[("qxi", s2)])
            P.add("pool", lambda e: e.tensor_tensor(out=kz[s2][:, :, :], in0=qkrot[s2][:, 256:512].rearrange("p (h d) -> p h d", h=4),
                                                    in1=zT[:, :].unsqueeze(2).broadcast_to([128, 4, 64]), op=ALU.mult),
                  reads=[("qkrot", s2), "zT"], writes=[("kz", s2)])
        elif part == 1:
            bs = nb()

            def mm_sc(e):
                r = None
                for h in range(4):
                    r = e.matmul(ps[bs][:, h * 128:(h + 1) * 128], lhsT=qkT[s2][:, 4 + h, :], rhs=qkT[s2][:, h, :], start=True, stop=True)
                return r
            P.add("pe", mm_sc, reads=[("qkT", s2)], writes=[PK(bs)], cost=0.5)
            P.add("dve", lambda e: e.tensor_tensor(out=scm[s2][:, :], in0=ps[bs][:, :], in1=decayT[:, :], op=ALU.mult),
                  reads=[PK(bs), "decayT"], writes=[("scm", s2)])
        else:
            bo = BO[t % 2]
            sbi = t % 2

            def mm_o(e):
                r = None
                for h in range(4):
                    r = e.matmul(ps[bo][:, h * 128:(h + 1) * 128], lhsT=scm[s2][:, h * 128:(h + 1) * 128],
                                 rhs=v_r[s2][:, h * 128:(h + 1) * 128], start=True, stop=(t == 0))
                    if t > 0:
                        r = e.matmul(ps[bo][:, h * 128:(h + 1) * 128], lhsT=qxi[s2][:, h, :],
                                     rhs=state_bf[sbi][:, h * 128:(h + 1) * 128], start=False, stop=True)
                return r
            P.add("pe", mm_o, reads=[("scm", s2), ("v_r", s2), ("qxi", s2)] + ([("state_bf", sbi)] if t > 0 else []), writes=[PK(bo)], cost=1.0)
            if t < NT - 1:
                bc = nb()

                def mm_c(e):
                    r = None
                    for h in range(4):
                        r = e.matmul(ps[bc][0:64, h * 128:(h + 1) * 128], lhsT=kz[s2][:, h, :], rhs=v_r[s2][:, h * 128:(h + 1) * 128],
                                     start=True, stop=True)
                    return r
                P.add("pe", mm_c, reads=[("kz", s2), ("v_r", s2)], writes=[PK(bc)], cost=0.5)
                if t == 0:
                    P.add("dve", lambda e: e.tensor_copy(out=state[:, :], in_=ps[bc][0:64, :]), reads=[PK(bc)], writes=["state"])
                else:
                    P.add("pool", lambda e: e.tensor_tensor(out=state[:, :], in0=state[:, :], in1=gcT[:, :], op=ALU.mult),
                          reads=["state", "gcT"], writes=["state"])
                    P.add("dve", lambda e: e.tensor_tensor(out=state[:, :], in0=state[:, :], in1=ps[bc][0:64, :], op=ALU.add),
                          reads=["state", PK(bc)], writes=["state"])
                P.add("act", lambda e: e.copy(out=state_bf[1 - sbi][:, :], in_=state[:, :]), reads=["state"], writes=[("state_bf", 1 - sbi)])

    def S4a(t):
        c, tt = divmod(t, 4)
        cs_ = c % 2
        s2 = t % 2
        bo = BO[t % 2]
        for h in range(4):
            P.add("dve", lambda e, h=h: e.bn_stats(st4[s2][:, h, :], ps[bo][:, h * 128:(h + 1) * 128]), reads=[PK(bo)], writes=[("st4", s2, h)])
            P.add("dve", lambda e, h=h: e.bn_aggr(mv4[s2][:, h, :], st4[s2][:, h, :]), reads=[("st4", s2, h)], writes=[("mv4", s2, h)])
        P.add("act", lambda e: e.activation(out=rstd4[s2][:, :], in_=mv4[s2][:, :, 1], func=AF.Sqrt, bias=epsc[:, 1:2], scale=1.0),
              reads=[("mv4", s2, h) for h in range(4)] + ["epsc"], writes=[("rstd4", s2)])
        P.add("dve", lambda e: e.reciprocal(out=rstd4[s2][:, :], in_=rstd4[s2][:, :]), reads=[("rstd4", s2)], writes=[("rstd4", s2)])
        for h in range(4):
            P.add("dve", lambda e, h=h: e.tensor_scalar(out=on[s2][:, h * 128:(h + 1) * 128], in0=ps[bo][:, h * 128:(h + 1) * 128],
                                                        scalar1=mv4[s2][:, h, 0:1], scalar2=rstd4[s2][:, h:h + 1],
                                                        op0=ALU.subtract, op1=ALU.mult),
                  reads=[PK(bo), ("mv4", s2, h), ("rstd4", s2)], writes=[("on", s2, h)])
        ONK = [("on", s2, h) for h in range(4)]
        P.add("pool", lambda e: e.tensor_tensor(out=on[s2][:, :], in0=on[s2][:, :], in1=gng[:, :], op=ALU.mult),
              reads=ONK + ["gng"], writes=ONK)
        P.add("pool", lambda e: e.tensor_tensor(out=on[s2][:, :], in0=on[s2][:, :], in1=gnb[:, :], op=ALU.add),
              reads=ONK + ["gnb"], writes=ONK)
        P.add("pool", lambda e: e.tensor_tensor(out=r_t[s2][:, :], in0=on[s2][:, :], in1=sg[s2][:, :], op=ALU.mult),
              reads=ONK + [("sg", s2)], writes=[("r_t", s2)])

    def S4b(t):
        c, tt = divmod(t, 4)
        cs_ = c % 2
        s2 = t % 2
        br = nb()

        def trr(e):
            r = None
            for j in range(4):
                r = e.transpose(psb[br][:, j * 128:(j + 1) * 128], r_t[s2][:, j * 128:(j + 1) * 128], ident[:, :])
            return r
        P.add("pe", trr, reads=[("r_t", s2), "ident"], writes=[PK(br)], cost=0.5)
        P.add("act", lambda e: e.copy(out=rT[:, :, t * 128:(t + 1) * 128], in_=psb[br][:, 0:512].rearrange("p (j t) -> p j t", j=4)),
              reads=[PK(br)], writes=[("rT", t)])


    BO = (6, 7)
    resv.update(BO)
    for t in range(4):
        ln_tile(t)
        if t < 3:
            ln_tr(t)
    for step in range(NT + 2):
        if step + 3 < NT:
            ln_tr(step + 3)
        t3, t4 = step - 1, step - 2
        if 0 <= t3 < NT:
            S3(t3, 0)
        if step < NT:
            if step % 4 == 0:
                fm_chunk(step // 4)
            S2(step, 0)
        if 0 <= t3 < NT:
            S3(t3, 1)
        if 0 <= t4 < NT:
            S4a(t4)
        if step < NT:
            S2(step, 1)
        if 0 <= t3 < NT:
            S3(t3, 2)
        if step + 4 < NT:
            ln_tile(step + 4)
        if 0 <= t4 < NT:
            S4b(t4)
    resv.clear()

    fin = []
    if stage == 2:
        P.add("sp", lambda e: e.dma_start(out=dbg["rT"].ap(), in_=rT[:, :, :]), reads=[("rT", t) for t in range(NT)], writes=["dbg_rT"], dma=True)
        P.add("sp", lambda e: e.nop(), reads=["dbg_rT"], writes=["fin"])
        P.finalize_and_emit()
        return nc, dbg

    P.barrier()
    sbW.reset(w0)
    QT = sbW.alloc([96, 8, S], BF, "QT")
    KT = sbW.alloc([96, 8, S], BF, "KT")
    V = sbW.alloc([128, NT, 8, 65], BF, "V")
    PT = [sbW.alloc([128, 512], BF, "PT%d" % i) for i in range(4)]
    Qtok = [sbW.alloc([128, 8, 96], BF, "Qtok%d" % i) for i in range(2)]
    qt1 = sbW.alloc([128, 256], F32, "qt1")
    qt2 = sbW.alloc([128, 256], F32, "qt2")
    rden = [sbW.alloc([128, 512], F32, "rden%d" % i) for i in range(2)]
    o_sb = [sbW.alloc([64, 512], F32, "o_sb%d" % i) for i in range(2)]
    P.add("pool", lambda e: e.memset(V[:, :, :, 64:65], 1.0), writes=["Vones"])
    QCN = lambda mc, c: ("qcnT", mc, c)
    for c in range(4):
        for h in range(8):
            bank = nb()
            P.add("pe", lambda e, h=h, bank=bank, c=c: e.matmul(ps[bank][0:64, :], lhsT=w_ukv[:, h * 64:(h + 1) * 64],
                                                                  rhs=qcnT[:, 2, c * 512:(c + 1) * 512], start=True, stop=True),
                  reads=["w_ukv", QCN(2, c)], writes=[PK(bank)], cost=0.45)
            eng = "act" if h % 2 == 0 else "dve"
            if eng == "act":
                P.add("act", lambda e, h=h, bank=bank, c=c: e.copy(out=KT[0:64, h, c * 512:(c + 1) * 512], in_=ps[bank][0:64, :]),
                      reads=[PK(bank)], writes=[("KTn", h, c)])
            else:
                P.add("dve", lambda e, h=h, bank=bank, c=c: e.tensor_copy(out=KT[0:64, h, c * 512:(c + 1) * 512], in_=ps[bank][0:64, :]),
                      reads=[PK(bank)], writes=[("KTn", h, c)])
        for tt in range(4):
            t = c * 4 + tt
            s2 = t % 2
            bank = nb()
            P.add("pe", lambda e, t=t, bank=bank: e.matmul(ps[bank][:, :], lhsT=qcnT[:, 2, t * 128:(t + 1) * 128], rhs=w_ukv[:, 512:1024],
                                                           start=True, stop=True), reads=["w_ukv", QCN(2, c)], writes=[PK(bank)], cost=0.45)
            P.add("act", lambda e, t=t, bank=bank: e.copy(out=V[:, t, :, 0:64], in_=ps[bank][:, :].rearrange("p (h d) -> p h d", h=8)),
                  reads=[PK(bank)], writes=[("V", t)])
            for g in range(2):
                bank = nb()

                def mmq(e, t=t, bank=bank, g=g):
                    r = None
                    for k2 in range(2):
                        r = e.matmul(ps[bank][:, 0:384], lhsT=qcnT[:, k2, t * 128:(t + 1) * 128], rhs=w_uq[:, k2, g * 384:(g + 1) * 384],
                                     start=(k2 == 0), stop=(k2 == 1))
                    return r
                P.add("pe", mmq, reads=["w_uq", QCN(0, c), QCN(1, c)], writes=[PK(bank)], cost=0.7)
                qv = ps[bank][:, 0:384].rearrange("p (h d) -> p h d", h=4)
                P.add("act", lambda e, qv=qv, s2=s2, g=g: e.copy(out=Qtok[s2][:, g * 4:(g + 1) * 4, 0:64], in_=qv[:, :, 0:64]),
                      reads=[PK(bank)], writes=[("Qtok", s2, g, 0)])
                rotary(qv[:, :, 64:96].rearrange("p h (two d) -> p h two d", two=2), cs16[:, t, 0, :], cs16[:, t, 1, :],
                       Qtok[s2][:, g * 4:(g + 1) * 4, 64:96].rearrange("p h (two d) -> p h two d", two=2), 4, 16, qt1, qt2,
                       [PK(bank)] + ROT16, ("Qtok", s2, g, 1), "qt")
            bank = nb()

            def trq(e, bank=bank, s2=s2):
                r = None
                for h in range(8):
                    r = e.transpose(psb[bank][0:96, h * 128:(h + 1) * 128], Qtok[s2][:, h, :], ident[:, :])
                return r
            P.add("pe", trq, reads=[("Qtok", s2, g, i) for g in range(2) for i in range(2)] + ["ident"], writes=[PK(bank)], cost=0.9)
            P.add("act", lambda e, bank=bank, t=t: e.copy(out=QT[:, :, t * 128:(t + 1) * 128],
                                                          in_=psb[bank][0:96, :].rearrange("p (h t) -> p h t", h=8)),
                  reads=[PK(bank)], writes=[("QT", t)])
    for half in range(2):
        bank = nb()

        def trk(e, bank=bank, half=half):
            r = None
            for j in range(8):
                r = e.transpose(psb[bank][0:96, j * 128:(j + 1) * 128], krot96[:, half * 8 + j, :], ident[:, :])
            return r
        P.add("pe", trk, reads=[("krot96", half * 8 + j) for j in range(8)] + ["krot96z", "ident"], writes=[PK(bank)], cost=0.9)
        for h in range(8):
            if h % 2 == 0:
                P.add("act", lambda e, bank=bank, half=half, h=h: e.copy(out=KT[64:96, h, half * 1024:(half + 1) * 1024], in_=psb[bank][64:96, :]),
                      reads=[PK(bank)], writes=[("KTr", h, half)])
            else:
                P.add("dve", lambda e, bank=bank, half=half, h=h: e.tensor_copy(out=KT[64:96, h, half * 1024:(half + 1) * 1024], in_=psb[bank][64:96, :]),
                      reads=[PK(bank)], writes=[("KTr", h, half)])
    SC = float(96.0 ** -0.5)
    NPT = len(PT)
    steps = []
    for h in range(8):
        for c in range(4):
            for j in range(4 * c + 4):
                steps.append((h, c, j, 4 * c + 4))
    LA = 2
    SBANK, ABANK, BBANK = (0, 1, 2, 3), (4, 5), (6, 7)

    def geom(c, j):
        q0 = max(c * 512, j * 128)
        return q0, (c + 1) * 512 - q0, q0 - c * 512

    def att_qk(i):
        h, c, j, nj = steps[i]
        q0, n, lo = geom(c, j)
        bs_ = SBANK[i % 4]
        pi = i % NPT
        kreads = [("KTn", h, j // 4), ("KTr", h, j // 8)] + [("QT", tq) for tq in range(q0 // 128, (c + 1) * 4)]
        P.add("pe", lambda e: e.matmul(ps[bs_][:, lo:lo + n], lhsT=KT[:, h, j * 128:(j + 1) * 128], rhs=QT[:, h, q0:q0 + n],
                                       start=True, stop=True), reads=kreads, writes=[PK(bs_)], cost=0.3)
        P.add("act", lambda e: e.activation(out=PT[pi][:, lo:lo + n], in_=ps[bs_][:, lo:lo + n], func=AF.Exp, scale=SC),
              reads=[PK(bs_)], writes=[("PT", pi)], cost=0.7)
        if j >= 4 * c:
            P.add("pool", lambda e: e.tensor_tensor(out=PT[pi][:, lo:lo + 128], in0=PT[pi][:, lo:lo + 128], in1=mask01[:, :], op=ALU.mult),
                  reads=[("PT", pi), "mask01"], writes=[("PT", pi)], cost=0.4)

    def att_pv(i):
        h, c, j, nj = steps[i]
        q0, n, lo = geom(c, j)
        pi = i % NPT
        hc = h * 4 + c
        bo = ABANK[hc % 2]
        P.add("pe", lambda e: e.matmul(ps[bo][0:65, lo:lo + n], lhsT=V[:, j, h, :], rhs=PT[pi][:, lo:lo + n],
                                       start=(j == 0), stop=(j == nj - 1)),
              reads=[("V", j), "Vones", ("PT", pi)], writes=[PK(bo)], cost=0.3)
        if j == nj - 1:
            s2 = hc % 2
            bb = BBANK[hc % 2]
            P.add("dve", lambda e: e.reciprocal(out=rden[s2][64:65, :], in_=ps[bo][64:65, :]), reads=[PK(bo)], writes=[("rden", s2)], cost=3.6)
            P.add("act", lambda e: e.copy(out=o_sb[s2][:, :], in_=ps[bo][0:64, :]), reads=[PK(bo)], writes=[("o_sb", s2)])
            P.add("pe", lambda e: e.matmul(ps[bb][0:64, :], lhsT=ones_f[64:65, 0:64], rhs=rden[s2][64:65, :], start=True, stop=True),
                  reads=[("rden", s2), "onesf"], writes=[PK(bb)])
            po = 64 * (h % 2)
            P.add("dve", lambda e: e.tensor_tensor(out=aT[po:po + 64, h // 2, c * 512:(c + 1) * 512], in0=o_sb[s2][:, :],
                                                   in1=ps[bb][0:64, :], op=ALU.mult),
                  reads=[("o_sb", s2), PK(bb)], writes=[("aT", h, c)])

    wa3 = [sbW.alloc([128, 8, 512], BF, "wa3_%d" % i) for i in range(2)]
    modrow3 = [sbW.alloc([1, 512], F32, "modrow3_%d" % i) for i in range(2)]
    badc3 = [sbW.alloc([1, 512], BF, "badc3_%d" % i) for i in range(2)]
    mod_dma(4, wa3, badc3)
    mod_dma(5, wa3, badc3)
    nmod = [4]
    for i in range(32):
        P.add("pool", lambda e, i=i: e.dma_start(out=w2bf_d.ap()[i * 128:(i + 1) * 128, :], in_=wff2_d.ap()[i * 128:(i + 1) * 128, :]),
              writes=[("w2bf", i)], dma=True)
    for i in range(8):
        P.add("pool", lambda e, i=i: e.dma_start(out=w1bf_d.ap()[i * 128:(i + 1) * 128, :], in_=wff1_d.ap()[i * 128:(i + 1) * 128, :]),
              writes=[("w1bf", i)], dma=True)
    for i in range(len(steps) + LA):
        if i < len(steps):
            att_qk(i)
        if i >= LA:
            att_pv(i - LA)
        if i % 32 == 20 and nmod[0] < 12:
            n = nmod[0]
            nmod[0] += 1
            mod_chunk(n, wa3, modrow3, badc3, BBANK[n % 2])
            if n + 2 < 12:
                mod_dma(n + 2, wa3, badc3)
    assert nmod[0] == 12

    if stage == 3:
        P.add("sp", lambda e: e.dma_start(out=dbg["aT"].ap(), in_=aT[:, :, :]), reads=[("aT", h, c) for h in range(8) for c in range(4)],
              writes=["dbg_aT"], dma=True)
        P.add("sp", lambda e: e.dma_start(out=dbg["KT"].ap(), in_=KT[:, :, :].rearrange("p a b -> p (a b)")), writes=["dbg_KT"], dma=True)
        P.add("sp", lambda e: e.dma_start(out=dbg["QT"].ap(), in_=QT[:, :, :].rearrange("p a b -> p (a b)")), writes=["dbg_QT"], dma=True)
        P.add("sp", lambda e: e.dma_start(out=dbg["V"].ap(), in_=V[:, :, :, :].rearrange("p a b c -> p (a b c)")), writes=["dbg_V"], dma=True)
        P.add("sp", lambda e: e.dma_start(out=dbg["krot"].ap(), in_=krot96[:, :, :].rearrange("p a b -> p (a b)")), writes=["dbg_krot"], dma=True)
        P.add("sp", lambda e: e.nop(), reads=["dbg_aT"], writes=["fin"])
        P.finalize_and_emit()
        return nc, dbg

    P.barrier()
    sbW.reset(w0)
    h2T = nc.alloc_sbuf_tensor_at("h2T", [128, 8, S], BF, offset=R_H)
    w_out = sbW.alloc([128, 8, D], BF, "w_out")
    wout_v = wout_d.ap().rearrange("(kc p) n -> p kc n", p=128)
    gt1 = sbW.alloc([128, D], F32, "gt1")
    wstg = [sbW.alloc([128, D], F32, "wstg%d" % i) for i in range(2)]
    g1 = sbW.alloc([128, D], F32, "g1")
    b1 = sbW.alloc([128, D], F32, "b1")
    G2 = sbW.alloc([128, D], F32, "G2")
    H2 = sbW.alloc([128, D], F32, "H2")
    htmp_p4 = sbW.alloc([128, D], F32, "htmp4")
    load_bcast(gt1, "gt1", mod_ap(2))
    for kc in range(8):
        ws_ = kc % 2
        P.add("sp", lambda e, kc=kc, ws_=ws_: e.dma_start(out=wstg[ws_][:, :], in_=wout_v[:, kc, :]), writes=[("wstg", ws_)], dma=True)
        P.add("dve" if kc % 2 else "pool", lambda e, kc=kc, ws_=ws_: e.tensor_tensor(out=w_out[:, kc, :], in0=wstg[ws_][:, :], in1=gt1[:, :], op=ALU.mult),
              reads=[("wstg", ws_), "gt1"], writes=[("w_out", kc)])
    load_bcast(g1, "g1", vec_d["ln1_g"].ap())
    load_bcast(b1, "b1", vec_d["ln1_b"].ap())
    load_bcast(H2, "H2", mod_ap(3))
    load_bcast(G2, "G2", mod_ap(4))
    P.add("dve", lambda e: e.tensor_tensor(out=htmp_p4[:, :], in0=b1[:, :], in1=G2[:, :], op=ALU.mult), reads=["b1", "G2"], writes=["htmp"])
    P.add("dve", lambda e: e.tensor_tensor(out=H2[:, :], in0=htmp_p4[:, :], in1=H2[:, :], op=ALU.add), reads=["htmp", "H2"], writes=["H2"])
    P.add("dve", lambda e: e.tensor_tensor(out=G2[:, :], in0=g1[:, :], in1=G2[:, :], op=ALU.mult), reads=["g1", "G2"], writes=["G2"])
    NS4 = 3
    x0l = [sbW.alloc([128, D], F32, "x0l%d" % i) for i in range(NS4)]
    zt_p4 = [sbW.alloc([128, D], F32, "zt%d" % i) for i in range(NS4)]
    x1t = [sbW.alloc([128, D], F32, "x1t%d" % i) for i in range(NS4)]
    ht_p4 = [sbW.alloc([128, D], BF, "ht4%d" % i) for i in range(NS4)]
    st_p4 = [sbW.alloc([128, 12], F32, "st4_%d" % i) for i in range(NS4)]
    mv_p4 = [sbW.alloc([128, 4], F32, "mv4_%d" % i) for i in range(NS4)]
    WOUT = [("w_out", kc) for kc in range(8)]

    def resid_ln(t, s2, banks, xl, xkey, gt, gtk, zt, st, mv, aff=None):
        ZK = [("zt", s2, 0), ("zt", s2, 1)]
        if gt is None:
            for hf in range(2):
                P.add("dve", lambda e, hf=hf: e.scalar_tensor_tensor(out=zt[s2][:, hf * 512:(hf + 1) * 512], in0=xl[:, hf * 512:(hf + 1) * 512],
                                                                     scalar=ALPHA, in1=ps[banks[hf]][:, :], op0=ALU.mult, op1=ALU.add),
                      reads=[PK(banks[hf]), xkey], writes=[("zt", s2, hf)])
        else:
            for hf in range(2):
                P.add("dve", lambda e, hf=hf: e.tensor_tensor(out=zt[s2][:, hf * 512:(hf + 1) * 512], in0=ps[banks[hf]][:, :],
                                                              in1=gt[:, hf * 512:(hf + 1) * 512], op=ALU.mult),
                      reads=[PK(banks[hf]), gtk], writes=[("zt", s2, hf)])
        if gt is None:
            pass
        elif aff is None:
            P.add("dve", lambda e: e.scalar_tensor_tensor(out=zt[s2][:, :], in0=xl[:, :], scalar=ALPHA, in1=zt[s2][:, :],
                                                           op0=ALU.mult, op1=ALU.add), reads=ZK + [xkey], writes=ZK)
        else:
            Ab = aff
            P.add("dve", lambda e: e.scalar_tensor_tensor(out=zt[s2][:, :], in0=xl[:, :], scalar=ALPHA, in1=zt[s2][:, :],
                                                           op0=ALU.mult, op1=ALU.add), reads=ZK + [xkey], writes=ZK)
            P.add("dve", lambda e: e.tensor_tensor(out=zt[s2][:, :], in0=zt[s2][:, :], in1=Ab[:, :], op=ALU.add), reads=ZK + ["Ab1"], writes=ZK)
        P.add("dve", lambda e: e.bn_stats(st[s2][:, 0:6], zt[s2][:, 0:512]), reads=ZK, writes=[("st", s2)])
        P.add("dve", lambda e: e.bn_stats(st[s2][:, 6:12], zt[s2][:, 512:1024]), reads=ZK, writes=[("st", s2)])
        P.add("dve", lambda e: e.bn_aggr(mv[s2][:, 0:2], st[s2][:, :]), reads=[("st", s2), ("st", s2)], writes=[("mv", s2)])
        P.add("act", lambda e: e.activation(out=mv[s2][:, 2:3], in_=mv[s2][:, 1:2], func=AF.Sqrt, bias=epsc[:, 0:1], scale=1.0),
              reads=[("mv", s2), "epsc"], writes=[("mv", s2)])
        P.add("dve", lambda e: e.reciprocal(out=mv[s2][:, 2:3], in_=mv[s2][:, 2:3]), reads=[("mv", s2)], writes=[("mv", s2)])
        P.add("dve", lambda e: e.scalar_tensor_tensor(out=mv[s2][:, 3:4], in0=mv[s2][:, 0:1], scalar=-1.0, in1=mv[s2][:, 2:3],
                                                      op0=ALU.mult, op1=ALU.mult),
              reads=[("mv", s2), ("mv", s2)], writes=[("mv", s2)])
        return ZK

    def resid_norm(s2, zt, mv):
        ZK = [("zt", s2, 0), ("zt", s2, 1)]
        P.add("act", lambda e: e.activation(out=zt[s2][:, :], in_=zt[s2][:, :], func=AF.Identity, bias=mv[s2][:, 3:4], scale=mv[s2][:, 2:3]),
              reads=ZK + [("mv", s2), ("mv", s2)], writes=ZK)
        return ZK

    def p4_A(t):
        s2 = t % NS4
        P.add("sp", lambda e, t=t, s2=s2: e.dma_start(out=x0l[s2][:, :], in_=x0_d.ap()[t * 128:(t + 1) * 128, :]), writes=[("x0l", s2)], dma=True)
        banks = [nb(), nb()]
        for hf in range(2):
            def mmo(e, hf=hf, t=t, banks=banks):
                r = None
                for kc in range(8):
                    src = aT[:, kc, t * 128:(t + 1) * 128] if kc < 4 else rT[:, kc - 4, t * 128:(t + 1) * 128]
                    r = e.matmul(ps[banks[hf]][:, :], lhsT=src, rhs=w_out[:, kc, hf * 512:(hf + 1) * 512], start=(kc == 0), stop=(kc == 7))
                return r
            P.add("pe", mmo, reads=WOUT, writes=[PK(banks[hf])], cost=2.6)
        ZK = resid_ln(t, s2, banks, x0l[s2], ("x0l", s2), None, None, zt_p4, st_p4, mv_p4)

    def p4_B(t):
        s2 = t % NS4
        ZK = resid_norm(s2, zt_p4, mv_p4)
        P.add("dve", lambda e, s2=s2: e.tensor_tensor(out=x1t[s2][:, :], in0=zt_p4[s2][:, :], in1=g1[:, :], op=ALU.mult),
              reads=ZK + ["g1"], writes=[("x1t", s2)])
        P.add("sp", lambda e, t=t, s2=s2: e.dma_start(out=x1_d.ap()[t * 128:(t + 1) * 128, :], in_=x1t[s2][:, :]),
              reads=[("x1t", s2)], writes=[("x1_d", t)], dma=True)
        P.add("pool", lambda e, s2=s2: e.tensor_tensor(out=zt_p4[s2][:, :], in0=zt_p4[s2][:, :], in1=G2[:, :], op=ALU.mult),
              reads=ZK + ["G2"], writes=ZK)
        P.add("pool", lambda e, s2=s2: e.tensor_tensor(out=ht_p4[s2][:, :], in0=zt_p4[s2][:, :], in1=H2[:, :], op=ALU.add),
              reads=ZK + ["H2"], writes=[("ht", s2)])
        bank = nb()

        def tr(e, bank=bank, s2=s2):
            r = None
            for kc in range(8):
                r = e.transpose(psb[bank][:, kc * 128:(kc + 1) * 128], ht_p4[s2][:, kc * 128:(kc + 1) * 128], ident[:, :])
            return r
        P.add("pe", tr, reads=[("ht", s2), "ident"], writes=[PK(bank)], cost=0.9)
        P.add("act", lambda e, bank=bank, t=t: e.copy(out=h2T[:, :, t * 128:(t + 1) * 128], in_=psb[bank][:, :].rearrange("p (k t) -> p k t", k=8)),
              reads=[PK(bank)], writes=[("h2T", t)])


    p4_A(0)
    p4_A(1)
    for t in range(NT):
        if t + 2 < NT:
            p4_A(t + 2)
        p4_B(t)

    if stage == 4:
        P.add("sp", lambda e: e.dma_start(out=dbg["x1"].ap(), in_=x1_d.ap()), reads=[("x1_d", t) for t in range(NT)], writes=["dbg_x1"], dma=True)
        P.add("sp", lambda e: e.nop(), reads=["dbg_x1"], writes=["fin"])
        P.finalize_and_emit()
        return nc, dbg

    P.barrier()
    sb5 = SBAlloc(nc, R_R, SB_END)
    W2 = sb5.alloc([128, 32, D], BF, "W2")
    UT = sb5.alloc([128, 32, 512], BF, "UT")
    NW1 = 3
    W1b = [sb5.alloc([128, 8, 512], BF, "W1b%d" % i) for i in range(NW1)]
    gt2 = sb5.alloc([128, D], F32, "gt2")
    g2 = sb5.alloc([128, D], F32, "g2")
    b2 = sb5.alloc([128, D], F32, "b2")
    x1l = [sb5.alloc([128, D], F32, "x1l%d" % i) for i in range(2)]
    zt_p5 = [sb5.alloc([128, D], F32, "zt5%d" % i) for i in range(2)]
    rl = [sb5.alloc([128, 512], F32, "rl%d" % i) for i in range(2)]
    st_p5 = [sb5.alloc([128, 12], F32, "st5_%d" % i) for i in range(2)]
    mv_p5 = [sb5.alloc([128, 4], F32, "mv5_%d" % i) for i in range(2)]
    w1bf_v = w1bf_d.ap().rearrange("(kc p) n -> p kc n", p=128)

    def w1_dma(g):
        blk = g % 8
        ws = g % NW1
        P.add("sp", lambda e: e.dma_start(out=W1b[ws][:, :, :], in_=w1bf_v[:, :, blk * 512:(blk + 1) * 512]),
              reads=[("w1bf", i) for i in range(8)], writes=[("W1b", ws)], dma=True)

    def x1_dma(t):
        s2 = t % 2
        P.add("sp", lambda e: e.dma_start(out=x1l[s2][:, :], in_=x1_d.ap()[t * 128:(t + 1) * 128, :]),
              reads=[("x1_d", t)], writes=[("x1l", s2)], dma=True)

    w1_dma(0)
    w1_dma(1)
    Ab1 = sb5.alloc([128, D], F32, "Ab1")
    load_bcast(Ab1, "Ab1", vec_d["ln1_b"].ap())
    P.add("dve", lambda e: e.tensor_scalar(out=Ab1[:, :], in0=Ab1[:, :], scalar1=ALPHA, scalar2=None, op0=ALU.mult), reads=["Ab1"], writes=["Ab1"])
    load_bcast(gt2, "gt2", mod_ap(5))
    load_bcast(g2, "g2", vec_d["ln2_g"].ap())
    load_bcast(b2, "b2", vec_d["ln2_b"].ap())
    for f in range(32):
        P.add("sp", lambda e, f=f: e.dma_start(out=W2[:, f, :], in_=w2bf_d.ap()[f * 128:(f + 1) * 128, :]), reads=[("w2bf", f)], writes=[("W2", f)], dma=True)
    UTK = [("UT", f) for f in range(32)]
    W2K = [("W2", f) for f in range(32)]
    for pz in range(4):
        H2K = [("h2T", pz * 4 + i) for i in range(4)]
        x1_dma(pz * 4)
        for blk in range(8):
            g = pz * 8 + blk
            ws = g % NW1
            if g + 2 < 32:
                w1_dma(g + 2)
            for fi in range(4):
                f = blk * 4 + fi
                bank = nb()

                def mm1(e, ws=ws, fi=fi, bank=bank, pz=pz):
                    r = None
                    for kc in range(8):
                        r = e.matmul(ps[bank][:, :], lhsT=W1b[ws][:, kc, fi * 128:(fi + 1) * 128], rhs=h2T[:, kc, pz * 512:(pz + 1) * 512],
                                     start=(kc == 0), stop=(kc == 7))
                    return r
                P.add("pe", mm1, reads=[("W1b", ws)] + H2K, writes=[PK(bank)], cost=1.8)
                rs_ = f % 2
                P.add("act", lambda e, bank=bank, rs_=rs_: e.activation(out=rl[rs_][:, :], in_=ps[bank][:, :], func=AF.Relu),
                      reads=[PK(bank)], writes=[("rl", rs_)])
                eng = "pool" if f % 2 == 0 else "dve"
                P.add(eng, lambda e, rs_=rs_, f=f: e.tensor_tensor(out=UT[:, f, :], in0=rl[rs_][:, :], in1=rl[rs_][:, :], op=ALU.mult),
                      reads=[("rl", rs_)], writes=[("UT", f)])
        for tt in range(4):
            t = pz * 4 + tt
            s2 = t % 2
            if tt < 3:
                x1_dma(t + 1)
            banks = [nb(), nb()]
            for hf in range(2):
                def mm2(e, hf=hf, tt=tt, banks=banks):
                    r = None
                    for f in range(32):
                        r = e.matmul(ps[banks[hf]][:, :], lhsT=UT[:, f, tt * 128:(tt + 1) * 128], rhs=W2[:, f, hf * 512:(hf + 1) * 512],
                                     start=(f == 0), stop=(f == 31))
                    return r
                P.add("pe", mm2, reads=UTK + W2K, writes=[PK(banks[hf])], cost=7.0)
            resid_ln(t, s2, banks, x1l[s2], ("x1l", s2), gt2, "gt2", zt_p5, st_p5, mv_p5, aff=Ab1)
            ZK = resid_norm(s2, zt_p5, mv_p5)
            P.add("dve", lambda e, s2=s2: e.tensor_tensor(out=x1l[s2][:, :], in0=zt_p5[s2][:, :], in1=g2[:, :], op=ALU.mult),
                  reads=ZK + ["g2"], writes=[("x1l", s2)])
            P.add("pool", lambda e, s2=s2: e.tensor_tensor(out=x1l[s2][:, :], in0=x1l[s2][:, :], in1=b2[:, :], op=ALU.add),
                  reads=[("x1l", s2), "b2"], writes=[("x1l", s2)])
            P.add("sp", lambda e, t=t, s2=s2: e.dma_start(out=out_d.ap()[t * 128:(t + 1) * 128, :], in_=x1l[s2][:, :]),
                  reads=[("x1l", s2)], writes=[("out", t)], dma=True)
    P.add("sp", lambda e: e.nop(), reads=[("out", t) for t in range(NT)], writes=["fin"])
    P.finalize_and_emit()
    return nc, dbg


_CACHE = {}


def _consts():
    f32 = np.float32
    log_g = np.log(f32(1.0) - f32(2.0) ** (-5.0 - np.arange(4, dtype=f32))).astype(f32)
    idx = np.arange(128, dtype=f32)
    diff = idx[:, None] - idx[None, :]
    decay = np.where(diff >= 0, np.exp(log_g[:, None, None] * np.maximum(diff, 0.0)), 0.0).astype(f32)
    decayT = np.ascontiguousarray(decay.transpose(2, 0, 1)).reshape(128, 512)
    zeta = np.exp(log_g[:, None] * (127.0 - idx)).astype(f32)
    xi = np.exp(log_g[:, None] * (idx + 1.0)).astype(f32)
    gch = np.exp(log_g * 128.0).astype(f32)
    xiT = np.ascontiguousarray(np.broadcast_to(xi.reshape(1, 512), (64, 512))).astype(f32)
    gcT = np.ascontiguousarray(np.broadcast_to(np.repeat(gch, 128).reshape(1, 512), (64, 512))).astype(f32)
    zT = np.ascontiguousarray(zeta.T).astype(f32)
    inv32 = (f32(10000.0) ** (-np.arange(32, dtype=f32) / f32(32))).astype(f32).reshape(1, 32)
    inv16 = (f32(10000.0) ** (-np.arange(16, dtype=f32) / f32(16))).astype(f32).reshape(1, 16)
    k = np.arange(128)
    mask01 = (k[None, :] >= k[:, None]).astype(f32)
    return {"ident": np.eye(128, dtype=f32), "mask01": mask01, "decayT": decayT, "xiT": xiT, "zT": zT, "gcT": gcT,
            "inv32": inv32, "inv16": inv16}


def _prep_inputs(inputs):
    f = lambda a: np.ascontiguousarray(np.asarray(a, dtype=np.float32))
    wukv = f(inputs["w_ukv"][0]).reshape(128, 8, 2, 64)
    wukv_p = np.ascontiguousarray(wukv.transpose(0, 2, 1, 3)).reshape(128, 1024)
    shared = {
        "ln_in_g": f(inputs["ln_in_g"]).reshape(1, D), "ln_in_b": f(inputs["ln_in_b"]).reshape(1, D),
        "ln1_g": f(inputs["ln1_g"][0]).reshape(1, D), "ln1_b": f(inputs["ln1_b"][0]).reshape(1, D),
        "ln2_g": f(inputs["ln2_g"][0]).reshape(1, D), "ln2_b": f(inputs["ln2_b"][0]).reshape(1, D),
        "gn_g": f(inputs["ret_gn_g"][0]).reshape(1, 512), "gn_b": f(inputs["ret_gn_b"][0]).reshape(1, 512),
        "w_ada": f(inputs["w_ada"][0]), "b_ada": f(inputs["b_ada"][0]).reshape(1, 6 * D),
        "w_in": f(inputs["w_in"][0]), "w_uq": f(inputs["w_uq"][0]), "w_ukv": wukv_p,
        "w_out": f(inputs["w_out"][0]), "w_ff1": f(inputs["w_ff1"][0]), "w_ff2": f(inputs["w_ff2"][0]),
        "gqT": f(inputs["mla_q_norm"][0]).reshape(2, 128).T.copy(),
        "gkvT": f(inputs["mla_kv_norm"][0]).reshape(1, 128).T.copy(),
    }
    shared.update(_consts())
    maps = []
    for b in range(8):
        m = dict(shared)
        m["x"] = f(inputs["x"][b])
        m["cT"] = f(inputs["c"][b]).reshape(8, 128).T.copy()
        m["pos"] = np.ascontiguousarray(np.asarray(inputs["positions"][b], dtype=np.int32).reshape(NT, 128).T)
        maps.append(m)
    return maps


def run(inputs, stage=99):
    if stage not in _CACHE:
        _CACHE[stage] = build(stage)
    nc, dbg = _CACHE[stage]
    maps = _prep_inputs(inputs)
    return run_bass_kernel_spmd(nc, maps, core_ids=list(range(8)))


def kernel(**inputs):
    res = run(inputs)
    out = np.stack([np.asarray(r["out"]) for r in res.results], axis=0)
    return out.astype(np.float32)
```
